# Optimizing a Trainium2 kernel written in Bass

```python
import math
import jax, jax.numpy as jnp
from jax import lax
import numpy as np

D_MODEL = 1024
BATCH = 8
SEQ = 2048
DEPTH = 1
DEC_BATCH = 128
DEC_SEQ = 4
PAST_LEN = 16384
PAGE_SIZE = 128

D_MIX = D_MODEL
D_A = D_MIX // 2
D_B = D_MIX - D_A
H_A = 4
DK_A = D_A // H_A
DV_A = D_A // H_A
N_BLK_B = 8
BLK_B = D_B // N_BLK_B
CONV_W = 4
LRU_C = 8.0
D_FF = 2816
CHUNK = 64
EPS = 1e-6
D_IN = 4 * D_A + 2 * D_B

kernel_name = "hymba_hgrn2_rglru_macaron_step"


def rmsnorm(x, g):
    xf = x.astype(jnp.float32)
    y = xf * lax.rsqrt(jnp.mean(xf * xf, axis=-1, keepdims=True) + EPS)
    return (y * g.astype(jnp.float32)).astype(x.dtype)


def swiglu(x, wg, wu, wd):
    return (jax.nn.silu(x @ wg) * (x @ wu)) @ wd


def hgrn2_chunked(q, logf, k, v, s0):
    b, t, h, dk = q.shape
    dv = v.shape[-1]
    c = math.gcd(t, CHUNK)
    n = t // c

    def to_chunks(a):
        return a.astype(jnp.float32).reshape(b, n, c, h, a.shape[-1]).transpose(1, 0, 3, 2, 4)

    qc, gc, kc, vc = to_chunks(q), to_chunks(logf), to_chunks(k), to_chunks(v)
    causal = jnp.tril(jnp.ones((c, c), dtype=bool))[:, :, None]

    def step(s, inp):
        qi, gi, ki, vi = inp
        cum = jnp.cumsum(gi, axis=2)
        diff = cum[:, :, :, None, :] - cum[:, :, None, :, :]
        decay = jnp.exp(jnp.where(causal, diff, -jnp.inf))
        scores = jnp.einsum('bhtk,bhsk,bhtsk->bhts', qi, ki, decay)
        o = (jnp.einsum('bhts,bhsv->bhtv', scores, vi)
             + jnp.einsum('bhtk,bhkv->bhtv', qi * jnp.exp(cum), s))
        last = cum[:, :, -1:, :]
        s_new = (jnp.exp(last[:, :, 0, :])[..., None] * s
                 + jnp.einsum('bhsk,bhsv->bhkv', ki * jnp.exp(last - cum), vi))
        return s_new, o

    s_fin, o = lax.scan(step, s0.astype(jnp.float32), (qc, gc, kc, vc))
    o = o.transpose(1, 0, 3, 2, 4).reshape(b, t, h, dv)
    return o, s_fin


def causal_conv(x, buf, w, bias):
    t = x.shape[1]
    xp = jnp.concatenate([buf.astype(x.dtype), x], axis=1)
    y = bias + sum(xp[:, j:j + t] * w[j] for j in range(CONV_W))
    return y, xp[:, -(CONV_W - 1):]


def block_diag(x, w, bias):
    b, t, _ = x.shape
    xb = x.reshape(b, t, N_BLK_B, BLK_B)
    return (jnp.einsum('btni,nij->btnj', xb, w) + bias).reshape(b, t, D_B)


def rglru(x, h0, w_a, b_a, w_x, b_x, lam, is_start):
    xf = x.astype(jnp.float32)
    r = jax.nn.sigmoid(block_diag(xf, w_a.astype(jnp.float32), b_a.astype(jnp.float32)))
    i = jax.nn.sigmoid(block_diag(xf, w_x.astype(jnp.float32), b_x.astype(jnp.float32)))
    log_a = -LRU_C * r * jax.nn.softplus(-lam.astype(jnp.float32))
    a = jnp.exp(log_a)
    mult = jnp.sqrt(-jnp.expm1(2.0 * log_a))
    if is_start:
        mult = mult.at[:, 0].set(1.0)
    u = mult * i * xf
    if h0 is not None:
        u = u.at[:, 0].add(a[:, 0] * h0.astype(jnp.float32))

    def comb(c1, c2):
        a1, b1 = c1
        a2, b2 = c2
        return a1 * a2, a2 * b1 + b2

    _, h = lax.associative_scan(comb, (a, u), axis=1)
    return h, h[:, -1]


def mixer(xn, lb, w_in, hgrn_norm, conv_w, conv_b, lru_wa, lru_ba, lru_wx, lru_bx, lru_lambda, w_o,
          s_hgrn, h_lru, conv_buf, is_start):
    b, t, _ = xn.shape
    proj = xn @ w_in
    q, fz, iv, g, xb, yb = jnp.split(proj, [D_A, 2 * D_A, 3 * D_A, 4 * D_A, 4 * D_A + D_B], axis=-1)
    f = lb + (1.0 - lb) * jax.nn.sigmoid(fz.astype(jnp.float32))
    logf = jnp.log(f)
    k = 1.0 - f
    qh = jax.nn.silu(q).reshape(b, t, H_A, DK_A)
    o_a, s_new = hgrn2_chunked(qh, logf.reshape(b, t, H_A, DK_A), k.reshape(b, t, H_A, DK_A),
                               iv.reshape(b, t, H_A, DV_A), s_hgrn)
    o_a = rmsnorm(o_a, hgrn_norm.reshape(H_A, DV_A)).reshape(b, t, D_A)
    o_a = (o_a * jax.nn.sigmoid(g.astype(jnp.float32))).astype(xn.dtype)
    xc, buf_new = causal_conv(xb, conv_buf, conv_w, conv_b)
    h, h_last = rglru(xc, h_lru, lru_wa, lru_ba, lru_wx, lru_bx, lru_lambda, is_start)
    o_b = (h * jax.nn.gelu(yb.astype(jnp.float32), approximate=True)).astype(xn.dtype)
    y = jnp.concatenate([o_a, o_b], axis=-1) @ w_o
    return y, s_new, h_last, buf_new


def trunk(x, s_hgrn, h_lru, conv_buf, is_start, ffn1_norm, ffn1_wg, ffn1_wu, ffn1_wd, mix_norm, w_in,
          hgrn_lb, hgrn_norm, conv_w, conv_b, lru_wa, lru_ba, lru_wx, lru_bx, lru_lambda, w_o,
          ffn2_norm, ffn2_wg, ffn2_wu, ffn2_wd, final_norm):
    lb_all = jnp.cumsum(jax.nn.softmax(hgrn_lb.astype(jnp.float32), axis=0), axis=0)
    ns_h, ns_l, ns_c = [], [], []
    for l in range(DEPTH):
        x = x + 0.5 * swiglu(rmsnorm(x, ffn1_norm[l]), ffn1_wg[l], ffn1_wu[l], ffn1_wd[l])
        m, s_new, h_last, buf_new = mixer(
            rmsnorm(x, mix_norm[l]), lb_all[l], w_in[l], hgrn_norm[l], conv_w[l], conv_b[l],
            lru_wa[l], lru_ba[l], lru_wx[l], lru_bx[l], lru_lambda[l], w_o[l],
            s_hgrn[l], None if h_lru is None else h_lru[l], conv_buf[l], is_start)
        x = x + m
        x = x + 0.5 * swiglu(rmsnorm(x, ffn2_norm[l]), ffn2_wg[l], ffn2_wu[l], ffn2_wd[l])
        ns_h.append(s_new)
        ns_l.append(h_last)
        ns_c.append(buf_new)
    return rmsnorm(x, final_norm), jnp.stack(ns_h), jnp.stack(ns_l), jnp.stack(ns_c)


def setup_inputs(seed: int = 0) -> dict:
    key = jax.random.key(seed)
    ks = jax.random.split(key, 32)
    f32 = jnp.float32

    def nrm(k, shape, scale):
        return jax.random.normal(k, shape, f32) * scale

    u = jax.random.uniform(ks[20], (DEPTH, D_B), f32, 0.9, 0.999)
    a_base = u ** (1.0 / LRU_C)
    lru_lambda = jnp.log(a_base) - jnp.log1p(-a_base)
    return {
        "x_prompt": nrm(ks[0], (BATCH, SEQ, D_MODEL), 1.0),
        "x_sample": nrm(ks[1], (DEC_BATCH, DEC_SEQ, D_MODEL), 1.0),
        "state_hgrn": nrm(ks[2], (DEPTH, DEC_BATCH, H_A, DK_A, DV_A), 0.5),
        "state_lru": nrm(ks[3], (DEPTH, DEC_BATCH, D_B), 0.5),
        "state_conv": nrm(ks[4], (DEPTH, DEC_BATCH, CONV_W - 1, D_B), 1.0),
        "ffn1_norm": 1.0 + nrm(ks[5], (DEPTH, D_MODEL), 0.02),
        "ffn1_wg": nrm(ks[6], (DEPTH, D_MODEL, D_FF), D_MODEL ** -0.5),
        "ffn1_wu": nrm(ks[7], (DEPTH, D_MODEL, D_FF), D_MODEL ** -0.5),
        "ffn1_wd": nrm(ks[8], (DEPTH, D_FF, D_MODEL), D_FF ** -0.5),
        "mix_norm": 1.0 + nrm(ks[9], (DEPTH, D_MODEL), 0.02),
        "w_in": nrm(ks[10], (DEPTH, D_MODEL, D_IN), D_MODEL ** -0.5),
        "hgrn_lb": nrm(ks[11], (DEPTH + 1, D_A), 0.1),
        "hgrn_norm": 1.0 + nrm(ks[12], (DEPTH, D_A), 0.02),
        "conv_w": nrm(ks[13], (DEPTH, CONV_W, D_B), CONV_W ** -0.5),
        "conv_b": nrm(ks[14], (DEPTH, D_B), 0.01),
        "lru_wa": nrm(ks[15], (DEPTH, N_BLK_B, BLK_B, BLK_B), BLK_B ** -0.5),
        "lru_ba": nrm(ks[16], (DEPTH, N_BLK_B, BLK_B), 0.01),
        "lru_wx": nrm(ks[17], (DEPTH, N_BLK_B, BLK_B, BLK_B), BLK_B ** -0.5),
        "lru_bx": nrm(ks[18], (DEPTH, N_BLK_B, BLK_B), 0.01),
        "lru_lambda": lru_lambda,
        "w_o": nrm(ks[19], (DEPTH, D_MIX, D_MODEL), D_MIX ** -0.5),
        "ffn2_norm": 1.0 + nrm(ks[21], (DEPTH, D_MODEL), 0.02),
        "ffn2_wg": nrm(ks[22], (DEPTH, D_MODEL, D_FF), D_MODEL ** -0.5),
        "ffn2_wu": nrm(ks[23], (DEPTH, D_MODEL, D_FF), D_MODEL ** -0.5),
        "ffn2_wd": nrm(ks[24], (DEPTH, D_FF, D_MODEL), D_FF ** -0.5),
        "final_norm": 1.0 + nrm(ks[25], (D_MODEL,), 0.02),
    }


def reference(x_prompt, x_sample, state_hgrn, state_lru, state_conv, ffn1_norm, ffn1_wg, ffn1_wu, ffn1_wd,
              mix_norm, w_in, hgrn_lb, hgrn_norm, conv_w, conv_b, lru_wa, lru_ba, lru_wx, lru_bx, lru_lambda,
              w_o, ffn2_norm, ffn2_wg, ffn2_wu, ffn2_wd, final_norm):
    weights = (ffn1_norm, ffn1_wg, ffn1_wu, ffn1_wd, mix_norm, w_in, hgrn_lb, hgrn_norm, conv_w, conv_b,
               lru_wa, lru_ba, lru_wx, lru_bx, lru_lambda, w_o, ffn2_norm, ffn2_wg, ffn2_wu, ffn2_wd, final_norm)
    s0_p = jnp.zeros((DEPTH, BATCH, H_A, DK_A, DV_A), jnp.float32)
    buf0_p = jnp.zeros((DEPTH, BATCH, CONV_W - 1, D_B), x_prompt.dtype)
    y_prompt, hgrn_p, lru_p, conv_p = trunk(x_prompt, s0_p, None, buf0_p, True, *weights)
    y_sample, hgrn_s, lru_s, conv_s = trunk(x_sample, state_hgrn, state_lru, state_conv, False, *weights)
    return (y_prompt, y_sample, hgrn_p, lru_p, conv_p, hgrn_s, lru_s, conv_s)
```

```python
import os
import numpy as np
import concourse.bass as bass
import concourse.mybir as mybir
from concourse.bass_utils import run_bass_kernel_spmd

F32 = mybir.dt.float32
BF16 = mybir.dt.bfloat16
AF = mybir.ActivationFunctionType
ALU = mybir.AluOpType

D = 1024
DFF = 2816
NF = 22
KC = 8
NP = 1024
NS = 32
NT = 1056
G = 2
TL = 352
DIN = 3072
EPS = 1e-6
NCORES = 8
CFG = dict(wh=1, wl=1, order='hhll', streams=False, mode='both', skip=())

ID0, MK0, M20, RS0, SM0, NCONST = 0, 128, 192, 224, 1280, 1536
NV = 76


class Buf:
    __slots__ = ("name", "last_w", "readers", "excl")

    def __init__(self, name, excl=False):
        self.name = name
        self.last_w = None
        self.readers = {}
        self.excl = excl


class Sched:
    ENGS = ("pe", "act", "dve", "pool", "sp")

    def __init__(self, nc):
        self.nc = nc
        self.ops = {e: [] for e in self.ENGS}
        self.count = {e: 0 for e in self.ENGS}
        self.known = {e: {} for e in self.ENGS}
        self.dma_count = {}

    @staticmethod
    def _flat(bufs):
        out = []
        for b in bufs:
            if isinstance(b, (list, tuple)):
                out.extend(Sched._flat(b))
            else:
                out.append(b)
        return out

    def _deps(self, eng, reads, writes):
        waits = {}

        def add(k, v):
            if k == eng and eng in ("pe", "sp", "pool"):
                return
            if self.known[eng].get(k, 0) >= v:
                return
            if waits.get(k, 0) < v:
                waits[k] = v

        for b in reads:
            if b.last_w is not None:
                add(*b.last_w)
            if b.excl:
                for k, v in b.readers.items():
                    if k != eng:
                        add(k, v)
        for b in writes:
            if b.last_w is not None:
                add(*b.last_w)
            for k, v in b.readers.items():
                add(k, v)
        for k, v in waits.items():
            self.known[eng][k] = v
        return list(waits.items())

    def _commit(self, ev, reads, writes):
        k, v = ev
        for b in reads:
            if b.readers.get(k, 0) < v:
                b.readers[k] = v
        for b in writes:
            b.last_w = ev
            b.readers = {}

    def op(self, eng, fn, reads=(), writes=()):
        reads, writes = self._flat(reads), self._flat(writes)
        waits = self._deps(eng, reads, writes)
        self.count[eng] += 1
        ev = (eng, self.count[eng])
        self.ops[eng].append((waits, fn, None))
        self._commit(ev, reads, writes)
        return ev

    def dma(self, queue, key, fn, reads=(), writes=()):
        reads, writes = self._flat(reads), self._flat(writes)
        waits = self._deps(queue, reads, writes)
        key = "dma:" + key
        self.dma_count[key] = self.dma_count.get(key, 0) + 16
        ev = (key, self.dma_count[key])
        self.ops[queue].append((waits, fn, key))
        self._commit(ev, reads, writes)
        return ev

    @staticmethod
    def fence(new_bufs, old_bufs):
        ev = {}
        for b in Sched._flat(old_bufs):
            if b.last_w is not None:
                k, v = b.last_w
                ev[k] = max(ev.get(k, 0), v)
            for k, v in b.readers.items():
                ev[k] = max(ev.get(k, 0), v)
        for b in Sched._flat(new_bufs):
            for k, v in ev.items():
                b.readers[k] = max(b.readers.get(k, 0), v)

    def wait_all(self, eng, events):
        waits = {}
        for (k, v) in events:
            if waits.get(k, 0) < v:
                waits[k] = v
        self.ops[eng].append((list(waits.items()), None, None))

    def emit(self):
        nc = self.nc
        keys = [e for e in self.ENGS if self.ops[e]]
        semkeys = list(keys) + list(self.dma_count.keys())
        sems = {k: nc.alloc_semaphore(name="s_" + k.replace(":", "_")) for k in semkeys}
        handle = {"pe": "tensor", "act": "scalar", "dve": "vector", "pool": "gpsimd", "sp": "sync"}
        with nc.Block() as block:
            for e in keys:
                oplist = self.ops[e]

                def body(engh, oplist=oplist, e=e):
                    for waits, fn, dkey in oplist:
                        for k, v in waits:
                            engh.wait_ge(sems[k], v)
                        if fn is None:
                            continue
                        ins = fn(engh)
                        if dkey is not None:
                            ins.then_inc(sems[dkey], 16)
                        else:
                            ins.then_inc(sems[e], 1)

                getattr(block, handle[e])(body)


def unit_sequence(stage=99):
    seq = []
    for g in range(G):
        if stage >= 1 and not os.environ.get("KSKIPFFN"):
            for u in range(11):
                seq.append(("gu", 1, u))
            for o in range(8):
                seq.append(("d", 1, o))
        if stage >= 2:
            seq.append(("win", 1024, 512))
            for h in range(4):
                seq.append(("win", 512 + 128 * h, 128))
                seq.append(("win", 128 * h, 128))
                seq.append(("win", 1536 + 128 * h, 128))
            for j in range(4):
                seq.append(("win", 2048 + 128 * j, 128))
                seq.append(("win", 2560 + 128 * j, 128))
            for o in range(8):
                seq.append(("wo", o, 128))
        if stage >= 3:
            for u in range(11):
                seq.append(("gu", 2, u))
            for o in range(8):
                seq.append(("d", 2, o))
        if stage < 5:
            break
    return seq


def build_program(useq_in=None):
    nc = bass.Bass("TRN2", target_bir_lowering=False)
    S = Sched(nc)

    def din(name, shape):
        return nc.dram_tensor(name, list(shape), F32, kind="ExternalInput").ap()

    def dout(name, shape):
        return nc.dram_tensor(name, list(shape), F32, kind="ExternalOutput").ap()

    xT = din("xT", [G, 128, KC, NT])
    consts_d = din("consts", [128, NCONST])
    vecs_d = din("vecs", [128, NV])
    wts = {
        ("g", 1): din("wg1", [D, DFF]), ("u", 1): din("wu1", [D, DFF]), ("d", 1): din("wd1", [DFF, D]),
        ("g", 2): din("wg2", [D, DFF]), ("u", 2): din("wu2", [D, DFF]), ("d", 2): din("wd2", [DFF, D]),
    }
    win_d = din("win", [D, DIN])
    wo_d = din("wo", [D, D])
    lwa_d = din("lwa", [8, 64, 64])
    lwx_d = din("lwx", [8, 64, 64])
    shg_d = din("s_hgrn", [16, 4, 128, 128])
    slru_d = din("s_lruT", [128, 4, 16])
    sconv_d = din("s_convT", [128, 4, 16, 3])

    yT = dout("yT", [G, 128, KC, NT])
    hgp_d = dout("hgrn_p", [4, 128, 128])
    hgs_d = dout("hgrn_s", [16, 4, 128, 128])
    small_d = dout("small", [G, 4, 36, 128])

    out_events = []

    def sb(name, shape, dt=F32):
        return nc.alloc_sbuf_tensor(name, list(shape), dt)

    X = sb("X", [128, KC, NT]);            Xb = [Buf(f"X{i}") for i in range(KC)]
    XN = sb("XN", [128, KC, NT], BF16);    XNb = [Buf(f"XN{i}") for i in range(KC)]
    H = sb("H", [128, NF, NT], BF16);      Hb = [Buf(f"H{i}") for i in range(NF)]
    NSLOT = 4
    RING = [sb(f"ring{i}", [128, 4096], BF16) for i in range(NSLOT)]
    RINGb = [[Buf(f"ring{i}a"), Buf(f"ring{i}b")] for i in range(NSLOT)]
    T = [sb(f"T{i}", [128, NT]) for i in range(9)]
    Tb = [[Buf(f"T{i}a"), Buf(f"T{i}b")] for i in range(9)]
    CON = sb("CON", [128, NCONST]);        CONb = Buf("CON")
    VEC = sb("VEC", [128, NV]);            VECb = Buf("VEC")
    DV = sb("DV", [128, 40]);              DVb = Buf("DV");  DVCb = Buf("DVC")
    IDB = sb("IDB", [128, 128], BF16);     IDBb = Buf("IDB")
    ONESB = sb("ONESB", [128, 128], BF16); ONESb = Buf("ONES")
    BD = sb("BD", [128, 2, 4, 128], BF16); BDb = Buf("BD")
    SALL = sb("SALL", [128, 17, 128]);     SALLb = [Buf("SALLa"), Buf("SALLb")]
    SCAR = sb("SCAR", [128, 4, 128]);      SCARb = Buf("SCAR")
    S0 = sb("S0", [128, 8, 128]);          S0b = Buf("S0")
    S0BF = sb("S0BF", [128, 8, 128], BF16); S0BFb = Buf("S0BF")
    SOUT = sb("SOUT", [128, 8, 128]);      SOUTb = Buf("SOUT")
    XB = sb("XB", [128, NP + 3]);          XBb = [Buf("XBa"), Buf("XBb")]
    XBS = sb("XBS", [128, 8, 7]);          XBSb = Buf("XBS")
    CONVC = sb("CONVC", [128, 4, 3]);      CONVCb = Buf("CONVC")
    HC = sb("HC", [128, 4]);               HCb = Buf("HC")
    H0 = sb("H0", [128, 4, 16]);           H0b = Buf("H0")
    SCV = sb("SCV", [128, 4, 16, 3]);      SCVb = Buf("SCV")
    OUTF = sb("OUTF", [128, 36]);          OUTFb = Buf("OUTF")
    OUTT = sb("OUTT", [36, 128]);          OUTTb = Buf("OUTT")
    TMP8 = sb("TMP8", [128, 8]);           TMP8b = Buf("TMP8")
    KHM = sb("KHM", [128, 8, 32], BF16);   KHMb = Buf("KHM")
    ATS = sb("ATS", [32, 32], BF16);       ATSb = Buf("ATS")
    KTOKS = sb("KTOKS", [32, 8, 128], BF16); KTOKSb = Buf("KTOKS")

    OA = H
    def hb2(name):
        return [Buf(name + "a"), Buf(name + "b")]

    QT, QTb = H[:, 8, :], hb2("QT")
    KT, KTb = H[:, 9, :], hb2("KT")
    KH, KHb = H[:, 10, :], hb2("KH")
    OSQ, OSQb = H[:, 11, :], hb2("OSQ")
    XCB, XCBb = H[:, 12, :], hb2("XCB")
    VTOK = H[:, 13:18, :].rearrange("p a b -> p (a b)")[:, 0:4608].rearrange("p (a b) -> p a b", b=512)
    VTOKb = [Buf("VTOK")]
    SBF = H[:, 18:20, :].rearrange("p a b -> p (a b)")[:, 0:2048].rearrange("p (a b) -> p a b", b=128)
    SBFb = hb2("SBF")
    KTOK = H[:, 20, 0:1024].rearrange("p (a b) -> p a b", b=128)
    KTOKb = hb2("KTOK")
    AT = H[:, 21, 0:512].rearrange("p (a b) -> p a b", b=64)
    ATb = hb2("AT")
    MIXTMP = [QTb, KTb, KHb, OSQb, XCBb, VTOKb, SBFb, KTOKb, ATb]

    PS = nc.alloc_psum_tensor("PS", [128, 8, 512], F32)
    PSb = [Buf(f"PS{i}", excl=True) for i in range(8)]
    PQb = [[PSb[6]], [PSb[7]]]
    SETB = [PSb[0:3], PSb[3:6]]
    FLAT = [PS[:, 0:3, :].rearrange("p a b -> p (a b)"), PS[:, 3:6, :].rearrange("p a b -> p (a b)")]
    FFNV = [PS[:, 0:3, 0:TL], PS[:, 3:6, 0:TL]]
    MISC = [PS[:, 6, :], PS[:, 7, :]]
    MISCBF = [PS[:, 6, :].bitcast(BF16), PS[:, 7, :].bitcast(BF16)]

    def v3(ap):
        return ap.rearrange("p (t n) -> p t n", n=TL)

    def act(out, in_, func, R, W, scale=None, bias=None):
        kw = {}
        if scale is not None:
            kw["scale"] = scale
        if bias is not None:
            kw["bias"] = bias
        return S.op("act", lambda e: e.activation(out=out, in_=in_, func=func, **kw), R, W)

    def acopy(out, in_, R, W):
        return S.op("act", lambda e: e.copy(out=out, in_=in_), R, W)

    def tt(out, in0, in1, op, R, W):
        return S.op("dve", lambda e: e.tensor_tensor(out=out, in0=in0, in1=in1, op=op), R, W)

    def ts(out, in0, s1, s2, op0, op1, R, W):
        if op1 is None:
            return S.op("dve", lambda e: e.tensor_scalar(out=out, in0=in0, scalar1=s1, scalar2=None, op0=op0), R, W)
        return S.op("dve", lambda e: e.tensor_scalar(out=out, in0=in0, scalar1=s1, scalar2=s2, op0=op0, op1=op1), R, W)

    def stt(out, in0, scalar, in1, op0, op1, R, W):
        return S.op("dve", lambda e: e.scalar_tensor_tensor(out=out, in0=in0, scalar=scalar, in1=in1, op0=op0, op1=op1), R, W)

    def scan(out, d0, d1, init, R, W):
        return S.op("dve", lambda e: e.tensor_tensor_scan(out=out, data0=d0, data1=d1, initial=init,
                                                          op0=ALU.mult, op1=ALU.add), R, W)

    def vcopy(out, in_, R, W):
        return S.op("dve", lambda e: e.tensor_copy(out=out, in_=in_), R, W)

    def memset(ap, val, W):
        return S.op("dve", lambda e: e.memset(ap, val), (), W)

    def mm(out, lhsT, rhs, start, stop, R, W):
        return S.op("pe", lambda e: e.matmul(out, lhsT=lhsT, rhs=rhs, start=start, stop=stop), R, W)

    def tr(out, in_, ident, R, W):
        return S.op("pe", lambda e: e.transpose(out, in_, ident), R, W)

    def dma_sp(key, out, in_, R, W):
        return S.dma("sp", key, lambda e: e.dma_start(out=out, in_=in_), R, W)

    def dma_pool(key, out, in_, R, W):
        return S.dma("pool", key, lambda e: e.dma_start(out=out, in_=in_), R, W)

    import os
    recording = useq_in is None
    useq = [] if recording else useq_in
    ring_state = {"issued": 0, "next": 0}

    def issue_unit(i):
        kind, a, b = useq[i]
        s = i % NSLOT
        slot, sbuf = RING[s], RINGb[s]
        key = f"ring{s}"
        xdep = []
        if kind == "gu":
            c0 = 256 * b
            for m, nm in enumerate(("g", "u")):
                src = wts[(nm, a)].rearrange("(kc p) n -> p kc n", p=128)[:, :, c0:c0 + 256]
                dst = slot[:, m * 2048:(m + 1) * 2048].rearrange("p (kc n) -> p kc n", n=256)
                dma_pool(key + "ab"[m], dst, src, xdep, [sbuf[m]])
        elif kind == "d":
            src = wts[("d", a)].rearrange("(f p) n -> p f n", p=128)[:, :, 128 * b:128 * b + 128]
            dst = slot[:, 0:NF * 128].rearrange("p (f n) -> p f n", n=128)
            dma_pool(key + "a", dst, src, (), sbuf)
        elif kind == "win":
            src = win_d.rearrange("(kc p) n -> p kc n", p=128)[:, :, a:a + b]
            dst = slot[:, 0:KC * b].rearrange("p (kc n) -> p kc n", n=b)
            dma_pool(key + "a", dst, src, (), sbuf)
        elif kind == "wo":
            src = wo_d.rearrange("(kc p) n -> p kc n", p=128)[:, :, 128 * a:128 * a + 128]
            dst = slot[:, 0:KC * 128].rearrange("p (kc n) -> p kc n", n=128)
            dma_pool(key + "a", dst, src, (), sbuf)

    def ring_next(spec):
        i = ring_state["next"]
        if recording:
            useq.append(spec)
        assert useq[i] == spec, (useq[i], spec)
        while ring_state["issued"] < min(len(useq), i + NSLOT - 1):
            issue_unit(ring_state["issued"])
            ring_state["issued"] += 1
        ring_state["next"] += 1
        s = i % NSLOT
        return RING[s], RINGb[s]

    if not recording:
        while ring_state["issued"] < min(len(useq), NSLOT - 1):
            issue_unit(ring_state["issued"])
            ring_state["issued"] += 1
    dma_sp("con", CON[:], consts_d, (), [CONb])
    dma_sp("vec", VEC[:], vecs_d, (), [VECb])
    dma_sp("h0", H0[:], slru_d, (), [H0b])
    dma_sp("scv", SCV[:], sconv_d, (), [SCVb])

    memset(BD[:], 0.0, [BDb])
    for a_i, src in enumerate((lwa_d, lwx_d)):
        sv = src.rearrange("(j two) i o -> two i j o", two=2)
        dma_pool("bd", BD[0:64, a_i, :, 0:64], sv[0], (), [BDb])
        dma_pool("bd", BD[64:128, a_i, :, 64:128], sv[1], (), [BDb])

    vcopy(IDB[:], CON[:, ID0:ID0 + 128], [CONb], [IDBb])
    memset(ONESB[:], 1.0, [ONESb])
    memset(DV[:, 36:37], EPS, [DVCb])
    memset(DV[:, 37:38], 1.0, [DVCb])
    memset(DV[:, 38:39], 0.0, [DVCb])
    def derive_consts():
        tt(DV[:, 0:4], VEC[:, 32:36], VEC[:, 36:40], ALU.subtract, [VECb], [DVb])
        act(DV[:, 0:4], DV[:, 0:4], AF.Tanh, [DVb], [DVb], scale=0.5)
        ts(DV[:, 4:8], DV[:, 0:4], 0.25, 0.75, ALU.mult, ALU.add, [DVb], [DVb])
        ts(DV[:, 8:12], DV[:, 0:4], -0.25, 0.25, ALU.mult, ALU.add, [DVb], [DVb])
        ts(DV[:, 12:16], DV[:, 0:4], 0.25, -0.25, ALU.mult, ALU.add, [DVb], [DVb])
        ts(DV[:, 16:20], VEC[:, 40:44], 0.5, None, ALU.mult, None, [VECb], [DVb])
        ts(DV[:, 20:24], VEC[:, 64:68], 0.5, None, ALU.mult, None, [VECb], [DVb])
        ts(DV[:, 24:28], VEC[:, 68:72], 0.5, None, ALU.mult, None, [VECb], [DVb])
        act(DV[:, 28:32], VEC[:, 72:76], AF.Exp, [VECb], [DVb], scale=-1.0)
        act(DV[:, 28:32], DV[:, 28:32], AF.Ln, [DVb, DVCb], [DVb], scale=1.0, bias=DV[:, 37:38])
        ts(DV[:, 32:36], DV[:, 28:32], -4.0, None, ALU.mult, None, [DVb], [DVb])
        ts(DV[:, 28:32], DV[:, 28:32], -8.0, None, ALU.mult, None, [DVb], [DVb])

    C0 = lambda h: DV[:, 4 + h:5 + h]
    C1 = lambda h: DV[:, 8 + h:9 + h]
    NC1 = lambda h: DV[:, 12 + h:13 + h]
    HG = lambda h: DV[:, 16 + h:17 + h]
    HBA = lambda j: DV[:, 20 + j:21 + j]
    HBX = lambda j: DV[:, 24 + j:25 + j]
    CC = lambda j: DV[:, 28 + j:29 + j]
    HCC = lambda j: DV[:, 32 + j:33 + j]
    EPSC = DV[:, 36:37]
    ONEC = DV[:, 37:38]

    MASKT = CON[:, MK0:MK0 + 64]
    M2 = CON[0:32, M20:M20 + 32]
    RESET = CON[:, RS0:RS0 + NT]
    SMASK = CON[:, SM0:SM0 + 256].rearrange("p (b t) -> p b t", t=32)
    IDF = CON[:, ID0:ID0 + 128]

    def norm_sq_one(kc, src, srcb, sq, sqb, pset):
        act(sq, src, AF.Square, [srcb], [sqb])
        for t in range(3):
            mm(PS[:, 3 * pset + t, 0:TL], ONESB[:], sq[:, t * TL:(t + 1) * TL], kc == 0, kc == KC - 1,
               [ONESb, sqb], [SETB[pset][t]])

    def norm_rstd(pset, rt, rtb):
        act(v3(rt), FFNV[pset], AF.Ln, SETB[pset] + [DVCb], [rtb], scale=1.0 / D, bias=EPSC)
        act(rt, rt, AF.Exp, [rtb], [rtb], scale=-0.5)

    def norm_apply_one(gcol, kc, src, srcb, rt, rtb, out, outbufs):
        stt(out, src, VEC[:, gcol + kc:gcol + kc + 1], rt, ALU.mult, ALU.mult, [srcb, VECb, rtb], outbufs)

    def square_ahead(kc):
        act(XN[:, kc, :], X[:, kc, :], AF.Square, [Xb[kc]], [XNb[kc]])

    def rmsnorm(gcol, out_fn, out_bufs_fn, pset, presq=False):
        for kc in range(KC):
            if presq:
                for t in range(3):
                    mm(PS[:, 3 * pset + t, 0:TL], ONESB[:], XN[:, kc, t * TL:(t + 1) * TL], kc == 0, kc == KC - 1,
                       [ONESb, XNb[kc]], [SETB[pset][t]])
                continue
            sq, sqb = (H[:, 11, :], Hb[11]) if kc % 2 == 0 else (H[:, 12, :], Hb[12])
            norm_sq_one(kc, X[:, kc, :], Xb[kc], sq, sqb, pset)
        norm_rstd(pset, T[3][:], Tb[3])
        for kc in range(KC):
            norm_apply_one(gcol, kc, X[:, kc, :], Xb[kc], T[3][:], Tb[3], out_fn(kc), out_bufs_fn(kc))

    def load_x(g, dst_fn, dst_bufs_fn, keypfx):
        for kc in range(KC):
            dma_sp(f"{keypfx}{kc}", dst_fn(kc), xT[g, :, kc, :], (), dst_bufs_fn(kc))

    def store_y(g):
        for kc in range(KC):
            out_events.append(dma_sp(f"y{kc}", yT[g, :, kc, :], X[:, kc, :], [Xb[kc]], ()))

    def ffn(which, gcol, pre_normed=False, hook1=None, hook2=None, presq=False, sq_ahead=False, late=False):
        if late:
            def xg(kc):
                if kc % 2 == 0:
                    act(XN[:, kc, :], X[:, kc, :], AF.Identity, [Xb[kc], VECb], [XNb[kc]],
                        scale=VEC[:, gcol + kc:gcol + kc + 1])
                else:
                    ts(XN[:, kc, :], X[:, kc, :], VEC[:, gcol + kc:gcol + kc + 1], None, ALU.mult, None,
                       [Xb[kc], VECb], [XNb[kc]])

            for kc in range(KC):
                if presq:
                    for t in range(3):
                        mm(PS[:, 3 + t, 0:TL], ONESB[:], XN[:, kc, t * TL:(t + 1) * TL], kc == 0, kc == KC - 1,
                           [ONESb, XNb[kc]], [SETB[1][t]])
                else:
                    sq, sqb = (H[:, 11, :], Hb[11]) if kc % 2 == 0 else (H[:, 12, :], Hb[12])
                    norm_sq_one(kc, X[:, kc, :], Xb[kc], sq, sqb, 1)
                    xg(kc)
            if presq:
                for kc in range(KC):
                    xg(kc)
            norm_rstd(1, T[3][:], Tb[3])
        elif not pre_normed:
            rmsnorm(gcol, lambda kc: XN[:, kc, :], lambda kc: [XNb[kc]], 0, presq=presq)
        for u in range(11):
            if hook1 is not None:
                hook1(u)
            slot, sbufs = ring_next(("gu", which, u))
            for fi in range(2):
                f = 2 * u + fi
                for m in range(2):
                    wv = slot[:, m * 2048:(m + 1) * 2048].rearrange("p (kc n) -> p kc n", n=256)
                    for kc in range(KC):
                        for t in range(3):
                            mm(PS[:, 3 * m + t, 0:TL], wv[:, kc, fi * 128:(fi + 1) * 128],
                               XN[:, kc, t * TL:(t + 1) * TL], kc == 0, kc == KC - 1,
                               [sbufs[m], XNb[kc]], [SETB[m][t]])
                tb = f % 2
                if late:
                    tt(v3(T[tb][:]), FFNV[0], v3(T[3][:]), ALU.mult, SETB[0] + [Tb[3]], [Tb[tb]])
                    act(T[tb][:], T[tb][:], AF.Silu, [Tb[tb]], [Tb[tb]])
                    tt(v3(T[tb][:]), v3(T[tb][:]), FFNV[1], ALU.mult, [Tb[tb]] + SETB[1], [Tb[tb]])
                    tt(H[:, f, :], T[tb][:], T[3][:], ALU.mult, [Tb[tb], Tb[3]], [Hb[f]])
                else:
                    act(v3(T[tb][:]), FFNV[0], AF.Silu, SETB[0], [Tb[tb]])
                    tt(v3(H[:, f, :]), v3(T[tb][:]), FFNV[1], ALU.mult, [Tb[tb]] + SETB[1], [Hb[f]])
        for o in range(8):
            if hook2 is not None:
                hook2(o)
            slot, sbufs = ring_next(("d", which, o))
            wv = slot[:, 0:NF * 128].rearrange("p (f n) -> p f n", n=128)
            ps = o % 2
            for f in range(NF):
                for t in range(3):
                    mm(PS[:, 3 * ps + t, 0:TL], wv[:, f, :], H[:, f, t * TL:(t + 1) * TL], f == 0, f == NF - 1,
                       sbufs + [Hb[f]], [SETB[ps][t]])
            stt(v3(X[:, o, :]), FFNV[ps], 0.5, v3(X[:, o, :]), ALU.mult, ALU.add, SETB[ps] + [Xb[o]], [Xb[o]])
            if sq_ahead:
                square_ahead(o)

    def prep_next_group(g_next):
        def hook(o):
            if o == 2:
                load_x(g_next, lambda kc: T[1 + kc][:], lambda kc: [Tb[1 + kc]], "xs")
            elif o == 3:
                for kc in range(KC):
                    act(XN[:, kc, :], T[1 + kc][:], AF.Square, [Tb[1 + kc]], [XNb[kc]])
            elif o == 4:
                pset = (o + 1) % 2
                for kc in range(KC):
                    for t in range(3):
                        mm(PS[:, 3 * pset + t, 0:TL], ONESB[:], XN[:, kc, t * TL:(t + 1) * TL], kc == 0, kc == KC - 1,
                           [ONESb, XNb[kc]], [SETB[pset][t]])
                norm_rstd(pset, T[0][:], Tb[0])
            elif o == 5:
                for kc in range(KC):
                    norm_apply_one(0, kc, T[1 + kc][:], Tb[1 + kc], T[0][:], Tb[0], XN[:, kc, :], [XNb[kc]])
        return hook

    def finish_prev_group(g_prev):
        def hook(u):
            if u == 1:
                for kc in range(KC):
                    sq, sqb = (H[:, 20, :], Hb[20]) if kc % 2 == 0 else (H[:, 21, :], Hb[21])
                    norm_sq_one(kc, X[:, kc, :], Xb[kc], sq, sqb, 1)
                norm_rstd(1, T[3][:], Tb[3])
            elif u == 2:
                for kc in range(KC):
                    norm_apply_one(24, kc, X[:, kc, :], Xb[kc], T[3][:], Tb[3], X[:, kc, :], [Xb[kc]])
                store_y(g_prev)
                load_x(g_prev + 1, lambda kc: X[:, kc, :], lambda kc: [Xb[kc]], "x")
        return hook

    TILES = ((0, 512), (512, 512), (1024, 32))

    def proj_flat(wv, sbuf, pset):
        for kc in range(KC):
            for t, (c0, n) in enumerate(TILES):
                mm(FLAT[pset][:, c0:c0 + n], wv[:, kc, :], XN[:, kc, c0:c0 + n], kc == 0, kc == KC - 1,
                   sbuf + [XNb[kc]], [SETB[pset][t]])

    def wslot(slot, n):
        return slot[:, 0:KC * n].rearrange("p (kc n) -> p kc n", n=n)

    class Stop(Exception):
        pass

    sub = int(os.environ.get("KSUB", "99"))

    def chk(n):
        if sub < n:
            raise Stop()

    fine = int(os.environ.get("KFINE", "99"))

    def chkf(n):
        if fine < n:
            raise Stop()

    def run_chains(gens, weights=None):
        active = list(gens)
        w = {id(gen): (weights[k] if weights else 1) for k, gen in enumerate(gens)}
        while active:
            for gen in list(active):
                for _ in range(w[id(gen)]):
                    try:
                        next(gen)
                    except StopIteration:
                        active.remove(gen)
                        break

    def proj_half(wv, sbuf, pset, tiles):
        for kc in range(KC):
            for (cc0, n, bk) in tiles:
                mm(FLAT[pset][:, cc0:cc0 + n], wv[:, kc, :], XN[:, kc, cc0:cc0 + n], kc == 0, kc == KC - 1,
                   sbuf + [XNb[kc]], [SETB[pset][bk]])

    def head_chain(g, h, hf, sh):
        c0, c1 = (0, 512) if hf == 0 else (512, NT)
        p1 = c0 + 512
        ck0 = 8 * hf
        samp = hf == 1
        tiles = [(0, 512, 0)] if hf == 0 else [(512, 512, 1), (1024, 32, 2)]
        A = FLAT[0]
        AB = [PSb[0]] if hf == 0 else [PSb[1], PSb[2]]
        B6 = [PSb[6]]
        ubanks = (0, 6) if hf == 0 else (1, 2)
        t0, t1, t2, t3 = (T[i][:, c0:c1] for i in range(4))
        b0, b1, b2, b3 = ([Tb[i][hf]] for i in range(4))
        qt, kt, kh, osq = QT[:, c0:c1], KT[:, c0:c1], KH[:, c0:c1], OSQ[:, c0:c1]
        qtb, ktb, khb, osqb = [QTb[hf]], [KTb[hf]], [KHb[hf]], [OSQb[hf]]
        Ah = A[:, c0:c1]
        if samp:
            dma_sp("s0", S0[:], shg_d[8 * g:8 * g + 8, h].rearrange("b k v -> k b v"), (), [S0b])
        if hf == 0:
            sh["fz"] = ring_next(("win", 512 + 128 * h, 128))
        slot, sbuf = sh["fz"]
        proj_half(wslot(slot, 128), sbuf, 0, tiles)
        yield
        act(t0, Ah, AF.Tanh, AB, b0, scale=0.5)
        yield
        ts(t1, t0, C1(h), C0(h), ALU.mult, ALU.add, b0 + [DVb], b1)
        ts(t2, t0, NC1(h), C1(h), ALU.mult, ALU.add, b0 + [DVb], b2)
        yield
        act(t0, t1, AF.Ln, b1, b0)
        if hf == 0:
            sh["q"] = ring_next(("win", 128 * h, 128))
        slot, sbuf = sh["q"]
        proj_half(wslot(slot, 128), sbuf, 0, tiles)
        yield
        scan(t1, RESET[:, c0:c1], t0, 0.0, [CONb] + b0, b1)
        yield
        act(t0, t1, AF.Exp, b1, b0)
        act(t3, t1, AF.Exp, b1, b3, scale=-1.0)
        yield
        tt(t3, t2, t3, ALU.mult, b2 + b3, b3)
        yield
        e_p = T[0][:, c0:p1].rearrange("p (c t) -> p c t", t=64)
        dP = e_p[:, :, 63:64].to_broadcast([128, 8, 64])
        tt(KH[:, c0:p1].rearrange("p (c t) -> p c t", t=64), T[3][:, c0:p1].rearrange("p (c t) -> p c t", t=64),
           dP, ALU.mult, b3 + b0, khb)
        if samp:
            dS = T[0][:, NP:NT].rearrange("p (c t) -> p c t", t=4)[:, :, 3:4].to_broadcast([128, 8, 4])
            tt(KH[:, NP:NT].rearrange("p (c t) -> p c t", t=4), T[3][:, NP:NT].rearrange("p (c t) -> p c t", t=4),
               dS, ALU.mult, b3 + b0, khb)
        yield
        act(t2, Ah, AF.Tanh, AB, b2, scale=0.5)
        yield
        stt(t2, t2, 1.0, Ah, ALU.add, ALU.mult, b2 + AB, b2)
        yield
        stt(qt, t2, 0.5, t0, ALU.mult, ALU.mult, b2 + b0, qtb)
        if hf == 0:
            sh["g"] = ring_next(("win", 1536 + 128 * h, 128))
        slot, sbuf = sh["g"]
        proj_half(wslot(slot, 128), sbuf, 0, tiles)
        for bl in range(4):
            blk = 4 * hf + bl
            tr(MISCBF[0][:, blk * 128:(blk + 1) * 128], KH[:, blk * 128:(blk + 1) * 128], IDB[:],
               khb + [IDBb], B6)
        yield
        act(t1, Ah, AF.Tanh, AB, b1, scale=0.5)
        acopy(KTOK[:, 4 * hf:4 * hf + 4, :].rearrange("p a b -> p (a b)"),
              MISCBF[0][:, 512 * hf:512 * hf + 512], B6, [KTOKb[hf]])
        acopy(kt, t3, b3, ktb)
        if samp:
            tt(KHM[:], KH[:, NP:NT].unsqueeze(1).to_broadcast([128, 8, 32]), SMASK, ALU.mult, khb + [CONb], [KHMb])
        yield
        if samp:
            for b in range(8):
                tr(MISCBF[0][0:32, b * 128:(b + 1) * 128], KHM[:, b, :], IDB[:], [KHMb, IDBb], B6)
        yield
        if samp:
            acopy(KTOKS[:].rearrange("p a b -> p (a b)"), MISCBF[0][0:32, 0:1024], B6, [KTOKSb])
            acopy(S0BF[:].rearrange("p a b -> p (a b)"), S0[:].rearrange("p a b -> p (a b)"), [S0b], [S0BFb])
        if hf == 0:
            if g == 0:
                memset(SALL[:, 0, :], 0.0, [SALLb[0]])
            else:
                vcopy(SALL[:, 0, :], SCAR[:, h, :], [SCARb], [SALLb[0]])
        yield
        for cl in range(8 if "chain" not in CFG["skip"] else 0):
            c = ck0 + cl
            r0 = 64 * (c % 2)
            sl = ubanks[c % 2]
            pu = PS[:, sl, 0:128]
            mm(pu, KTOK[r0:r0 + 64, c // 2, :], VTOK[r0:r0 + 64, c // 2, 128 * h:128 * h + 128], True, True,
               [KTOKb[hf]] + VTOKb, [PSb[sl]])
            stt(SALL[:, c + 1, :], SALL[:, c, :], T[0][:, 64 * c + 63:64 * c + 64], pu, ALU.mult, ALU.add,
                SALLb + b0 + [PSb[sl]], [SALLb[hf]])
        yield
        acopy(SBF[:, ck0:ck0 + 8, :].rearrange("p a b -> p (a b)"),
              SALL[:, ck0:ck0 + 8, :].rearrange("p a b -> p (a b)"), SALLb, [SBFb[hf]])
        if samp:
            vcopy(SCAR[:, h, :], SALL[:, 16, :], [SALLb[1]], [SCARb])
            if g == G - 1:
                out_events.append(dma_sp("hgp", hgp_d[h], SCAR[:, h, :], [SCARb], ()))
        for cl in range(8):
            c = ck0 + cl
            r0 = 64 * (c % 2)
            mm(PS[r0:r0 + 64, 6, (c // 2) * 64:(c // 2) * 64 + 64], kt[:, 64 * cl:64 * cl + 64],
               qt[:, 64 * cl:64 * cl + 64], True, True, ktb + qtb, B6)
        yield
        tt(AT[:, 4 * hf:4 * hf + 4, :], PS[:, 6, 256 * hf:256 * hf + 256].rearrange("p (a b) -> p a b", b=64),
           MASKT.unsqueeze(1).to_broadcast([128, 4, 64]), ALU.mult, B6 + [CONb], [ATb[hf]])
        yield
        if samp:
            mm(PS[0:32, 6, 0:32], KT[:, NP:NT], QT[:, NP:NT], True, True, ktb + qtb, B6)
        yield
        if samp:
            tt(ATS[:], PS[0:32, 6, 0:32], M2, ALU.mult, B6 + [CONb], [ATSb])
        for cl in range(8):
            c = ck0 + cl
            r0 = 64 * (c % 2)
            bk = [PSb[0]] if hf == 0 else [PSb[1]]
            mm(A[:, 64 * c:64 * c + 64], VTOK[r0:r0 + 64, c // 2, 128 * h:128 * h + 128], AT[r0:r0 + 64, c // 2, :],
               True, False, VTOKb + [ATb[hf]], bk)
            mm(A[:, 64 * c:64 * c + 64], SBF[:, c, :], QT[:, 64 * c:64 * c + 64], False, True, [SBFb[hf]] + qtb, bk)
        yield
        if samp:
            mm(A[:, NP:NT], VTOK[0:32, 8, 128 * h:128 * h + 128], ATS[:], True, False, VTOKb + [ATSb], [PSb[2]])
            for b in range(8):
                mm(A[:, NP + 4 * b:NP + 4 * b + 4], S0BF[:, b, :], QT[:, NP + 4 * b:NP + 4 * b + 4], False, b == 7,
                   [S0BFb] + qtb, [PSb[2]])
        yield
        act(osq, Ah, AF.Square, AB, osqb)
        act(t3, Ah, AF.Identity, AB + [DVb], b3, scale=HG(h))
        yield
        for (cc0, n, bk) in tiles:
            mm(A[:, cc0:cc0 + n], ONESB[:], OSQ[:, cc0:cc0 + n], True, True, [ONESb] + osqb, [SETB[0][bk]])
        stt(t3, t1, 1.0, t3, ALU.add, ALU.mult, b1 + b3, b3)
        yield
        act(t2, Ah, AF.Ln, AB + [DVCb], b2, scale=1.0 / 128, bias=EPSC)
        yield
        act(t2, t2, AF.Exp, b2, b2, scale=-0.5)
        yield
        yield
        tt(OA[:, h, c0:c1], t3, t2, ALU.mult, b3 + b2, [Hb[h]])
        if samp:
            for b in range(8):
                mm(PS[:, 1 + b // 4, (b % 4) * 128:(b % 4) * 128 + 128], KTOKS[0:32, b, :],
                   VTOK[0:32, 8, 128 * h:128 * h + 128], True, True, [KTOKSb] + VTOKb, [PSb[1 + b // 4]])
            dB = T[0][:, NP:NT].rearrange("p (b t) -> p b t", t=4)[:, :, 3:4].to_broadcast([128, 8, 128])
            tt(SOUT[:], S0[:], dB, ALU.mult, [S0b] + b0, [SOUTb])
            tt(SOUT[:].rearrange("p b v -> p (b v)"), SOUT[:].rearrange("p b v -> p (b v)"),
               PS[:, 1:3, :].rearrange("p a b -> p (a b)"), ALU.add, [SOUTb, PSb[1], PSb[2]], [SOUTb])
            out_events.append(dma_sp("hgs", hgs_d[8 * g:8 * g + 8, h].rearrange("b k v -> k b v"), SOUT[:],
                                     [SOUTb], ()))

    def lru_chain(g, j, hf, sh):
        c0, c1 = (0, 512) if hf == 0 else (512, NT)
        p1 = c0 + 512
        samp = hf == 1
        tiles = [(0, 512, 0)] if hf == 0 else [(512, 512, 1), (1024, 32, 2)]
        Bf = FLAT[1]
        BB = [PSb[3]] if hf == 0 else [PSb[4], PSb[5]]
        B7 = [PSb[7]]
        l0, l1, l2, l3, l4 = (T[i][:, c0:c1] for i in range(4, 9))
        lb0, lb1, lb2, lb3, lb4 = ([Tb[i][hf]] for i in range(4, 9))
        L0, L1, L2, L4 = T[4], T[5], T[6], T[8]
        xcb, xcbb = XCB[:, c0:c1], [XCBb[hf]]
        Bh = Bf[:, c0:c1]
        xbw = [XBb[hf]]
        xbr = XBb
        if hf == 0:
            sh["xb"] = ring_next(("win", 2048 + 128 * j, 128))
        slot, sbuf = sh["xb"]
        proj_half(wslot(slot, 128), sbuf, 1, tiles)
        if hf == 0:
            if g == 0:
                memset(XB[:, 0:3], 0.0, [XBb[0]])
            else:
                vcopy(XB[:, 0:3], CONVC[:, j, :], [CONVCb], [XBb[0]])
        else:
            vcopy(XBS[:, :, 0:3], SCV[:, j, 8 * g:8 * g + 8, :], [SCVb], [XBSb])
        yield
        acopy(XB[:, 3 + c0:3 + p1], Bf[:, c0:p1], BB, xbw)
        if samp:
            acopy(XBS[:, :, 3:7], Bf[:, NP:NT].rearrange("p (b t) -> p b t", t=4), BB, [XBSb])
        yield
        if samp:
            vcopy(CONVC[:, j, :], XB[:, NP:NP + 3], xbr, [CONVCb])
        cw = lambda tap: VEC[:, 44 + 4 * j + tap:45 + 4 * j + tap]
        ts(L4[:, c0:p1], XB[:, 3 + c0:3 + p1], cw(3), VEC[:, 60 + j:61 + j], ALU.mult, ALU.add, xbr + [VECb], lb4)
        L4s = L4[:, NP:NT].rearrange("p (b t) -> p b t", t=4)
        if samp:
            ts(L4s, XBS[:, :, 3:7], cw(3), VEC[:, 60 + j:61 + j], ALU.mult, ALU.add, [XBSb, VECb], lb4)
        yield
        for tap in range(3):
            stt(L4[:, c0:p1], XB[:, tap + c0:tap + p1], cw(tap), L4[:, c0:p1], ALU.mult, ALU.add,
                xbr + [VECb] + lb4, lb4)
            if samp:
                stt(L4s, XBS[:, :, tap:tap + 4], cw(tap), L4s, ALU.mult, ALU.add, [XBSb, VECb] + lb4, lb4)
            yield
        acopy(xcb, l4, lb4, xcbb)
        yield
        for (cc0, n, bk) in tiles:
            mm(Bf[:, cc0:cc0 + n], BD[:, 0, j, :], XCB[:, cc0:cc0 + n], True, True, [BDb] + xcbb, [SETB[1][bk]])
        yield
        act(l0, Bh, AF.Tanh, BB + [DVb], lb0, scale=0.5, bias=HBA(j))
        yield
        for (cc0, n, bk) in tiles:
            mm(Bf[:, cc0:cc0 + n], BD[:, 1, j, :], XCB[:, cc0:cc0 + n], True, True, [BDb] + xcbb, [SETB[1][bk]])
        act(l2, l0, AF.Exp, lb0 + [DVb], lb2, scale=HCC(j), bias=HCC(j))
        act(l3, l0, AF.Exp, lb0 + [DVb], lb3, scale=CC(j), bias=CC(j))
        yield
        act(l1, Bh, AF.Tanh, BB + [DVb], lb1, scale=0.5, bias=HBX(j))
        if hf == 0:
            sh["yb"] = ring_next(("win", 2560 + 128 * j, 128))
        slot, sbuf = sh["yb"]
        proj_half(wslot(slot, 128), sbuf, 1, tiles)
        yield
        act(l3, l3, AF.Ln, lb3 + [DVb, DVCb], lb3, scale=-1.0, bias=ONEC)
        stt(l1, l1, 1.0, l4, ALU.add, ALU.mult, lb1 + lb4, lb1)
        yield
        act(l3, l3, AF.Exp, lb3, lb3, scale=0.5)
        if g == 0 and hf == 0:
            memset(T[7][:, 0:1], 1.0, lb3)
        yield
        stt(l1, l1, 0.5, l3, ALU.mult, ALU.mult, lb1 + lb3, lb1)
        a_s = L2[:, NP:NT].rearrange("p (b t) -> p b t", t=4)
        u_s = L1[:, NP:NT].rearrange("p (b t) -> p b t", t=4)
        if samp:
            tt(TMP8[:].unsqueeze(2), a_s[:, :, 0:1], H0[:, j, 8 * g:8 * g + 8].unsqueeze(2), ALU.mult,
               lb2 + [H0b], [TMP8b])
        yield
        if samp:
            tt(u_s[:, :, 0:1], u_s[:, :, 0:1], TMP8[:].unsqueeze(2), ALU.add, lb1 + [TMP8b], lb1)
            memset(a_s[:, :, 0:1], 0.0, lb2)
        yield
        if hf == 0:
            init, initb = (0.0, []) if g == 0 else (HC[:, j:j + 1], [HCb])
        else:
            init, initb = L0[:, 511:512], [Tb[4][0]]
        scan(L0[:, c0:p1], L2[:, c0:p1], L1[:, c0:p1], init, lb2 + lb1 + initb, lb0)
        if samp:
            scan(L0[:, NP:NT], L2[:, NP:NT], L1[:, NP:NT], 0.0, lb2 + lb1, lb0)
        yield
        if samp:
            vcopy(HC[:, j:j + 1], L0[:, NP - 1:NP], lb0, [HCb])
        PY = Bh
        act(l2, PY, AF.Square, BB, lb2)
        yield
        ts(l2, l2, 0.044715, 1.0, ALU.mult, ALU.add, lb2, lb2)
        yield
        tt(l2, l2, PY, ALU.mult, lb2 + BB, lb2)
        yield
        act(l2, l2, AF.Tanh, lb2, lb2, scale=0.7978845608028654)
        if samp:
            vcopy(OUTF[:, 0:24].rearrange("p (b t) -> p b t", t=3), XBS[:, :, 4:7], [XBSb], [OUTFb])
            vcopy(OUTF[:, 24:32].unsqueeze(2), L0[:, NP:NT].rearrange("p (b t) -> p b t", t=4)[:, :, 3:4],
                  lb0, [OUTFb])
            vcopy(OUTF[:, 32:35], XB[:, NP:NP + 3], xbr, [OUTFb])
            vcopy(OUTF[:, 35:36], L0[:, NP - 1:NP], lb0, [OUTFb])
        yield
        if samp:
            tr(PS[0:36, 7, 0:128], OUTF[:], IDF, [OUTFb, CONb], B7)
        stt(l2, l2, 1.0, PY, ALU.add, ALU.mult, lb2 + BB, lb2)
        yield
        if samp:
            acopy(OUTT[:], PS[0:36, 7, 0:128], B7, [OUTTb])
            out_events.append(dma_sp("small", small_d[g, j], OUTT[:], [OUTTb], ()))
        stt(OA[:, 4 + j, c0:c1], l2, 0.5, l0, ALU.mult, ALU.mult, lb2 + lb0, [Hb[4 + j]])

    def mixer(g):
        rmsnorm(8, lambda kc: XN[:, kc, :], lambda kc: [XNb[kc]], 0, presq=True)
        Sched.fence(MIXTMP, Hb[8:22])
        slot, sbuf = ring_next(("win", 1024, 512))
        wv = wslot(slot, 512)
        for kc in range(KC):
            for blk in range(8):
                mm(PS[:, blk, :], XN[:, kc, blk * 128:blk * 128 + 128], wv[:, kc, :], kc == 0, kc == KC - 1,
                   sbuf + [XNb[kc]], [PSb[blk]])
        for blk in range(8):
            acopy(VTOK[:, blk, :], PS[:, blk, :], [PSb[blk]], VTOKb)
        for kc in range(KC):
            mm(PS[0:32, 0, :], XN[:, kc, NP:NT], wv[:, kc, :], kc == 0, kc == KC - 1, sbuf + [XNb[kc]], [PSb[0]])
        acopy(VTOK[0:32, 8, :], PS[0:32, 0, :], [PSb[0]], VTOKb)
        def build(hgens, lgens):
            wh, wl = CFG["wh"], CFG["wl"]
            if CFG["mode"] == "head":
                return hgens, [wh, wh]
            if CFG["mode"] == "lru":
                return lgens, [wl, wl]
            if CFG["mode"] == "none":
                return [], []
            o = CFG["order"]
            if o == "hhll":
                return hgens + lgens, [wh, wh, wl, wl]
            if o == "llhh":
                return lgens + hgens, [wl, wl, wh, wh]
            if o == "hlhl":
                return [hgens[0], lgens[0], hgens[1], lgens[1]], [wh, wl, wh, wl]
            raise ValueError(o)

        if CFG["streams"]:
            shh = [{} for _ in range(4)]
            shl = [{} for _ in range(4)]

            def seq(fn, hf, shs):
                for i in range(4):
                    yield from fn(g, i, hf, shs[i])

            gens, ws = build([seq(head_chain, 0, shh), seq(head_chain, 1, shh)],
                             [seq(lru_chain, 0, shl), seq(lru_chain, 1, shl)])
            run_chains(gens, ws)
        else:
            for i in range(4):
                shh, shl = {}, {}
                gens, ws = build([head_chain(g, i, 0, shh), head_chain(g, i, 1, shh)],
                                 [lru_chain(g, i, 0, shl), lru_chain(g, i, 1, shl)])
                run_chains(gens, ws)
        Sched.fence(Hb[8:22], MIXTMP)
        chk(15)
        for o in range(8):
            slot, sbuf = ring_next(("wo", o, 128))
            wv = wslot(slot, 128)
            ps = o % 2
            for kc in range(KC):
                for t in range(3):
                    mm(PS[:, 3 * ps + t, 0:TL], wv[:, kc, :], OA[:, kc, t * TL:(t + 1) * TL], kc == 0, kc == KC - 1,
                       sbuf + [Hb[kc]], [SETB[ps][t]])
            tt(v3(X[:, o, :]), FFNV[ps], v3(X[:, o, :]), ALU.add, SETB[ps] + [Xb[o]], [Xb[o]])
            square_ahead(o)

    load_x(0, lambda kc: X[:, kc, :], lambda kc: [Xb[kc]], "x")
    for g in range(G):
        last = (g == G - 1)
        if g == 0:
            ffn(1, 0, sq_ahead=True, late=True)
            derive_consts()
        else:
            ffn(1, 0, pre_normed=True, hook1=finish_prev_group(g - 1), sq_ahead=True)
        mixer(g)
        ffn(2, 16, hook2=None if last else prep_next_group(g + 1), presq=True, sq_ahead=last, late=True)
    rmsnorm(24, lambda kc: X[:, kc, :], lambda kc: [Xb[kc]], 0, presq=True)
    store_y(G - 1)
    S.wait_all("sp", out_events)
    if recording:
        return useq
    S.emit()
    return nc


def _consts():
    c = np.zeros((128, NCONST), np.float32)
    c[:, ID0:ID0 + 128] = np.eye(128, dtype=np.float32)
    p = np.arange(128)[:, None] % 64
    t = np.arange(64)[None, :]
    c[:, MK0:MK0 + 64] = (p <= t).astype(np.float32)
    s = np.arange(32)[:, None]
    t = np.arange(32)[None, :]
    c[0:32, M20:M20 + 32] = ((s // 4 == t // 4) & (s <= t)).astype(np.float32)
    r = np.ones(NT, np.float32)
    r[0:NP:64] = 0.0
    r[NP:NT:4] = 0.0
    c[:, RS0:RS0 + NT] = r[None, :]
    b = np.arange(8)[:, None]
    tok = np.arange(32)[None, :]
    c[:, SM0:SM0 + 256] = (tok // 4 == b).astype(np.float32).reshape(1, 256)
    return c


def _fm(v, n):
    return np.ascontiguousarray(np.asarray(v, np.float32).reshape(n, 128).T)


_NC_CACHE = {}


def prepare(x_prompt, x_sample, state_hgrn, state_lru, state_conv, ffn1_norm, ffn1_wg, ffn1_wu, ffn1_wd,
            mix_norm, w_in, hgrn_lb, hgrn_norm, conv_w, conv_b, lru_wa, lru_ba, lru_wx, lru_bx, lru_lambda,
            w_o, ffn2_norm, ffn2_wg, ffn2_wu, ffn2_wd, final_norm):
    f32 = np.float32
    x_prompt = np.asarray(x_prompt, f32)
    x_sample = np.asarray(x_sample, f32)
    state_hgrn = np.asarray(state_hgrn, f32)
    state_lru = np.asarray(state_lru, f32)
    state_conv = np.asarray(state_conv, f32)

    vecs = np.zeros((128, NV), f32)
    vecs[:, 0:8] = _fm(ffn1_norm[0], 8)
    vecs[:, 8:16] = _fm(mix_norm[0], 8)
    vecs[:, 16:24] = _fm(ffn2_norm[0], 8)
    vecs[:, 24:32] = _fm(final_norm, 8)
    vecs[:, 32:36] = _fm(hgrn_lb[0], 4)
    vecs[:, 36:40] = _fm(hgrn_lb[1], 4)
    vecs[:, 40:44] = _fm(hgrn_norm[0], 4)
    cw = np.asarray(conv_w, f32)[0]
    for j in range(4):
        for tap in range(4):
            vecs[:, 44 + 4 * j + tap] = cw[tap, 128 * j:128 * j + 128]
    vecs[:, 60:64] = _fm(conv_b[0], 4)
    vecs[:, 64:68] = _fm(np.asarray(lru_ba, f32)[0].reshape(512), 4)
    vecs[:, 68:72] = _fm(np.asarray(lru_bx, f32)[0].reshape(512), 4)
    vecs[:, 72:76] = _fm(lru_lambda[0], 4)
    consts = _consts()

    shared = {
        "consts": consts, "vecs": vecs,
        "wg1": np.ascontiguousarray(ffn1_wg[0], f32), "wu1": np.ascontiguousarray(ffn1_wu[0], f32),
        "wd1": np.ascontiguousarray(ffn1_wd[0], f32),
        "wg2": np.ascontiguousarray(ffn2_wg[0], f32), "wu2": np.ascontiguousarray(ffn2_wu[0], f32),
        "wd2": np.ascontiguousarray(ffn2_wd[0], f32),
        "win": np.ascontiguousarray(w_in[0], f32), "wo": np.ascontiguousarray(w_o[0], f32),
        "lwa": np.ascontiguousarray(lru_wa[0], f32), "lwx": np.ascontiguousarray(lru_wx[0], f32),
    }
    in_maps = []
    for c in range(NCORES):
        xs = x_sample[16 * c:16 * c + 16]
        xt = np.empty((G, 128, KC, NT), f32)
        for g in range(G):
            tok = np.concatenate([x_prompt[c, g * NP:(g + 1) * NP], xs[8 * g:8 * g + 8].reshape(NS, D)], axis=0)
            xt[g] = tok.T.reshape(KC, 128, NT).transpose(1, 0, 2)
        m = dict(shared)
        m["xT"] = xt
        m["s_hgrn"] = np.ascontiguousarray(state_hgrn[0, 16 * c:16 * c + 16])
        m["s_lruT"] = np.ascontiguousarray(state_lru[0, 16 * c:16 * c + 16].T.reshape(4, 128, 16).transpose(1, 0, 2))
        sc = state_conv[0, 16 * c:16 * c + 16]
        m["s_convT"] = np.ascontiguousarray(sc.transpose(2, 0, 1).reshape(4, 128, 16, 3).transpose(1, 0, 2, 3))
        in_maps.append(m)
    return in_maps


def assemble(results):
    f32 = np.float32
    y_prompt = np.empty((8, 2048, D), f32)
    y_sample = np.empty((128, 4, D), f32)
    hgrn_p = np.empty((1, 8, 4, 128, 128), f32)
    lru_p = np.empty((1, 8, 512), f32)
    conv_p = np.empty((1, 8, 3, 512), f32)
    hgrn_s = np.empty((1, 128, 4, 128, 128), f32)
    lru_s = np.empty((1, 128, 512), f32)
    conv_s = np.empty((1, 128, 3, 512), f32)
    for c in range(NCORES):
        r = results[c]
        yt = np.asarray(r["yT"], f32)
        for g in range(G):
            tok = yt[g].transpose(1, 0, 2).reshape(D, NT).T
            y_prompt[c, g * NP:(g + 1) * NP] = tok[0:NP]
            y_sample[16 * c + 8 * g:16 * c + 8 * g + 8] = tok[NP:NT].reshape(8, 4, D)
        hgrn_p[0, c] = np.asarray(r["hgrn_p"], f32)
        hgrn_s[0, 16 * c:16 * c + 16] = np.asarray(r["hgrn_s"], f32)
        sm = np.asarray(r["small"], f32)
        for g in range(G):
            for j in range(4):
                blk = sm[g, j]
                fs = slice(128 * j, 128 * j + 128)
                conv_s[0, 16 * c + 8 * g:16 * c + 8 * g + 8, :, fs] = blk[0:24].reshape(8, 3, 128)
                lru_s[0, 16 * c + 8 * g:16 * c + 8 * g + 8, fs] = blk[24:32]
                if g == G - 1:
                    conv_p[0, c, :, fs] = blk[32:35]
                    lru_p[0, c, fs] = blk[35]
    return (y_prompt, y_sample, hgrn_p, lru_p, conv_p, hgrn_s, lru_s, conv_s)


def kernel(**inputs):
    in_maps = prepare(**inputs)
    if "nc" not in _NC_CACHE:
        _NC_CACHE["nc"] = build_program(build_program())
    nc = _NC_CACHE["nc"]
    res = run_bass_kernel_spmd(nc, in_maps, core_ids=list(range(NCORES)))
    return assemble(res.results)
```

```python
import os
import numpy as np
import concourse.bass as bass
import concourse.mybir as mybir
from concourse.bass_utils import run_bass_kernel_spmd

F32 = mybir.dt.float32
BF16 = mybir.dt.bfloat16
AF = mybir.ActivationFunctionType
ALU = mybir.AluOpType

D = 1024
DFF = 2816
NF = 22
KC = 8
NP = 1024
NS = 32
NT = 1056
G = 2
TL = 352
DIN = 3072
EPS = 1e-6
NCORES = 8
CFG = dict(wh=1, wl=1, order='hhll', streams=False, mode='both', skip=())

ID0, MK0, M20, RS0, SM0, NCONST = 0, 128, 192, 224, 1280, 1536
NV = 76


class Buf:
    __slots__ = ("name", "last_w", "readers", "excl")

    def __init__(self, name, excl=False):
        self.name = name
        self.last_w = None
        self.readers = {}
        self.excl = excl


class Sched:
    ENGS = ("pe", "act", "dve", "pool", "sp")

    def __init__(self, nc):
        self.nc = nc
        self.ops = {e: [] for e in self.ENGS}
        self.count = {e: 0 for e in self.ENGS}
        self.known = {e: {} for e in self.ENGS}
        self.dma_count = {}

    @staticmethod
    def _flat(bufs):
        out = []
        for b in bufs:
            if isinstance(b, (list, tuple)):
                out.extend(Sched._flat(b))
            else:
                out.append(b)
        return out

    def _deps(self, eng, reads, writes):
        waits = {}

        def add(k, v):
            if k == eng and eng in ("pe", "sp", "pool"):
                return
            if self.known[eng].get(k, 0) >= v:
                return
            if waits.get(k, 0) < v:
                waits[k] = v

        for b in reads:
            if b.last_w is not None:
                add(*b.last_w)
            if b.excl:
                for k, v in b.readers.items():
                    if k != eng:
                        add(k, v)
        for b in writes:
            if b.last_w is not None:
                add(*b.last_w)
            for k, v in b.readers.items():
                add(k, v)
        for k, v in waits.items():
            self.known[eng][k] = v
        return list(waits.items())

    def _commit(self, ev, reads, writes):
        k, v = ev
        for b in reads:
            if b.readers.get(k, 0) < v:
                b.readers[k] = v
        for b in writes:
            b.last_w = ev
            b.readers = {}

    def op(self, eng, fn, reads=(), writes=()):
        reads, writes = self._flat(reads), self._flat(writes)
        waits = self._deps(eng, reads, writes)
        self.count[eng] += 1
        ev = (eng, self.count[eng])
        self.ops[eng].append((waits, fn, None))
        self._commit(ev, reads, writes)
        return ev

    def dma(self, queue, key, fn, reads=(), writes=()):
        reads, writes = self._flat(reads), self._flat(writes)
        waits = self._deps(queue, reads, writes)
        key = "dma:" + key
        self.dma_count[key] = self.dma_count.get(key, 0) + 16
        ev = (key, self.dma_count[key])
        self.ops[queue].append((waits, fn, key))
        self._commit(ev, reads, writes)
        return ev

    @staticmethod
    def fence(new_bufs, old_bufs):
        ev = {}
        for b in Sched._flat(old_bufs):
            if b.last_w is not None:
                k, v = b.last_w
                ev[k] = max(ev.get(k, 0), v)
            for k, v in b.readers.items():
                ev[k] = max(ev.get(k, 0), v)
        for b in Sched._flat(new_bufs):
            for k, v in ev.items():
                b.readers[k] = max(b.readers.get(k, 0), v)

    def wait_all(self, eng, events):
        waits = {}
        for (k, v) in events:
            if waits.get(k, 0) < v:
                waits[k] = v
        self.ops[eng].append((list(waits.items()), None, None))

    def emit(self):
        nc = self.nc
        keys = [e for e in self.ENGS if self.ops[e]]
        semkeys = list(keys) + list(self.dma_count.keys())
        sems = {k: nc.alloc_semaphore(name="s_" + k.replace(":", "_")) for k in semkeys}
        handle = {"pe": "tensor", "act": "scalar", "dve": "vector", "pool": "gpsimd", "sp": "sync"}
        with nc.Block() as block:
            for e in keys:
                oplist = self.ops[e]

                def body(engh, oplist=oplist, e=e):
                    for waits, fn, dkey in oplist:
                        for k, v in waits:
                            engh.wait_ge(sems[k], v)
                        if fn is None:
                            continue
                        ins = fn(engh)
                        if dkey is not None:
                            ins.then_inc(sems[dkey], 16)
                        else:
                            ins.then_inc(sems[e], 1)

                getattr(block, handle[e])(body)


def unit_sequence(stage=99):
    seq = []
    for g in range(G):
        if stage >= 1 and not os.environ.get("KSKIPFFN"):
            for u in range(11):
                seq.append(("gu", 1, u))
            for o in range(8):
                seq.append(("d", 1, o))
        if stage >= 2:
            seq.append(("win", 1024, 512))
            for h in range(4):
                seq.append(("win", 512 + 128 * h, 128))
                seq.append(("win", 128 * h, 128))
                seq.append(("win", 1536 + 128 * h, 128))
            for j in range(4):
                seq.append(("win", 2048 + 128 * j, 128))
                seq.append(("win", 2560 + 128 * j, 128))
            for o in range(8):
                seq.append(("wo", o, 128))
        if stage >= 3:
            for u in range(11):
                seq.append(("gu", 2, u))
            for o in range(8):
                seq.append(("d", 2, o))
        if stage < 5:
            break
    return seq


def build_program(useq_in=None):
    nc = bass.Bass("TRN2", target_bir_lowering=False)
    S = Sched(nc)

    def din(name, shape):
        return nc.dram_tensor(name, list(shape), F32, kind="ExternalInput").ap()

    def dout(name, shape):
        return nc.dram_tensor(name, list(shape), F32, kind="ExternalOutput").ap()

    xT = din("xT", [G, 128, KC, NT])
    consts_d = din("consts", [128, NCONST])
    vecs_d = din("vecs", [128, NV])
    wts = {
        ("g", 1): din("wg1", [D, DFF]), ("u", 1): din("wu1", [D, DFF]), ("d", 1): din("wd1", [DFF, D]),
        ("g", 2): din("wg2", [D, DFF]), ("u", 2): din("wu2", [D, DFF]), ("d", 2): din("wd2", [DFF, D]),
    }
    win_d = din("win", [D, DIN])
    wo_d = din("wo", [D, D])
    lwa_d = din("lwa", [8, 64, 64])
    lwx_d = din("lwx", [8, 64, 64])
    shg_d = din("s_hgrn", [16, 4, 128, 128])
    slru_d = din("s_lruT", [128, 4, 16])
    sconv_d = din("s_convT", [128, 4, 16, 3])

    yT = dout("yT", [G, 128, KC, NT])
    hgp_d = dout("hgrn_p", [4, 128, 128])
    hgs_d = dout("hgrn_s", [16, 4, 128, 128])
    small_d = dout("small", [G, 4, 36, 128])

    out_events = []

    def sb(name, shape, dt=F32):
        return nc.alloc_sbuf_tensor(name, list(shape), dt)

    X = sb("X", [128, KC, NT]);            Xb = [Buf(f"X{i}") for i in range(KC)]
    XN = sb("XN", [128, KC, NT], BF16);    XNb = [Buf(f"XN{i}") for i in range(KC)]
    H = sb("H", [128, NF, NT], BF16);      Hb = [Buf(f"H{i}") for i in range(NF)]
    NSLOT = 4
    RING = [sb(f"ring{i}", [128, 4096], BF16) for i in range(NSLOT)]
    RINGb = [[Buf(f"ring{i}a"), Buf(f"ring{i}b")] for i in range(NSLOT)]
    T = [sb(f"T{i}", [128, NT]) for i in range(9)]
    Tb = [[Buf(f"T{i}a"), Buf(f"T{i}b")] for i in range(9)]
    CON = sb("CON", [128, NCONST]);        CONb = Buf("CON")
    VEC = sb("VEC", [128, NV]);            VECb = Buf("VEC")
    DV = sb("DV", [128, 40]);              DVb = Buf("DV");  DVCb = Buf("DVC")
    IDB = sb("IDB", [128, 128], BF16);     IDBb = Buf("IDB")
    ONESB = sb("ONESB", [128, 128], BF16); ONESb = Buf("ONES")
    BD = sb("BD", [128, 2, 4, 128], BF16); BDb = Buf("BD")
    SALL = sb("SALL", [128, 17, 128]);     SALLb = [Buf("SALLa"), Buf("SALLb")]
    SCAR = sb("SCAR", [128, 4, 128]);      SCARb = Buf("SCAR")
    S0 = sb("S0", [128, 8, 128]);          S0b = Buf("S0")
    S0BF = sb("S0BF", [128, 8, 128], BF16); S0BFb = Buf("S0BF")
    SOUT = sb("SOUT", [128, 8, 128]);      SOUTb = Buf("SOUT")
    XB = sb("XB", [128, NP + 3]);          XBb = [Buf("XBa"), Buf("XBb")]
    XBS = sb("XBS", [128, 8, 7]);          XBSb = Buf("XBS")
    CONVC = sb("CONVC", [128, 4, 3]);      CONVCb = Buf("CONVC")
    HC = sb("HC", [128, 4]);               HCb = Buf("HC")
    H0 = sb("H0", [128, 4, 16]);           H0b = Buf("H0")
    SCV = sb("SCV", [128, 4, 16, 3]);      SCVb = Buf("SCV")
    OUTF = sb("OUTF", [128, 36]);          OUTFb = Buf("OUTF")
    OUTT = sb("OUTT", [36, 128]);          OUTTb = Buf("OUTT")
    TMP8 = sb("TMP8", [128, 8]);           TMP8b = Buf("TMP8")
    KHM = sb("KHM", [128, 8, 32], BF16);   KHMb = Buf("KHM")
    ATS = sb("ATS", [32, 32], BF16);       ATSb = Buf("ATS")
    KTOKS = sb("KTOKS", [32, 8, 128], BF16); KTOKSb = Buf("KTOKS")

    OA = H
    def hb2(name):
        return [Buf(name + "a"), Buf(name + "b")]

    QT, QTb = H[:, 8, :], hb2("QT")
    KT, KTb = H[:, 9, :], hb2("KT")
    KH, KHb = H[:, 10, :], hb2("KH")
    OSQ, OSQb = H[:, 11, :], hb2("OSQ")
    XCB, XCBb = H[:, 12, :], hb2("XCB")
    VTOK = H[:, 13:18, :].rearrange("p a b -> p (a b)")[:, 0:4608].rearrange("p (a b) -> p a b", b=512)
    VTOKb = [Buf("VTOK")]
    SBF = H[:, 18:20, :].rearrange("p a b -> p (a b)")[:, 0:2048].rearrange("p (a b) -> p a b", b=128)
    SBFb = hb2("SBF")
    KTOK = H[:, 20, 0:1024].rearrange("p (a b) -> p a b", b=128)
    KTOKb = hb2("KTOK")
    AT = H[:, 21, 0:512].rearrange("p (a b) -> p a b", b=64)
    ATb = hb2("AT")
    MIXTMP = [QTb, KTb, KHb, OSQb, XCBb, VTOKb, SBFb, KTOKb, ATb]

    PS = nc.alloc_psum_tensor("PS", [128, 8, 512], F32)
    PSb = [Buf(f"PS{i}", excl=True) for i in range(8)]
    PQb = [[PSb[6]], [PSb[7]]]
    SETB = [PSb[0:3], PSb[3:6]]
    FLAT = [PS[:, 0:3, :].rearrange("p a b -> p (a b)"), PS[:, 3:6, :].rearrange("p a b -> p (a b)")]
    FFNV = [PS[:, 0:3, 0:TL], PS[:, 3:6, 0:TL]]
    MISC = [PS[:, 6, :], PS[:, 7, :]]
    MISCBF = [PS[:, 6, :].bitcast(BF16), PS[:, 7, :].bitcast(BF16)]

    def v3(ap):
        return ap.rearrange("p (t n) -> p t n", n=TL)

    def act(out, in_, func, R, W, scale=None, bias=None):
        kw = {}
        if scale is not None:
            kw["scale"] = scale
        if bias is not None:
            kw["bias"] = bias
        return S.op("act", lambda e: e.activation(out=out, in_=in_, func=func, **kw), R, W)

    def acopy(out, in_, R, W):
        return S.op("act", lambda e: e.copy(out=out, in_=in_), R, W)

    def tt(out, in0, in1, op, R, W):
        return S.op("dve", lambda e: e.tensor_tensor(out=out, in0=in0, in1=in1, op=op), R, W)

    def ts(out, in0, s1, s2, op0, op1, R, W):
        if op1 is None:
            return S.op("dve", lambda e: e.tensor_scalar(out=out, in0=in0, scalar1=s1, scalar2=None, op0=op0), R, W)
        return S.op("dve", lambda e: e.tensor_scalar(out=out, in0=in0, scalar1=s1, scalar2=s2, op0=op0, op1=op1), R, W)

    def stt(out, in0, scalar, in1, op0, op1, R, W):
        return S.op("dve", lambda e: e.scalar_tensor_tensor(out=out, in0=in0, scalar=scalar, in1=in1, op0=op0, op1=op1), R, W)

    def scan(out, d0, d1, init, R, W):
        return S.op("dve", lambda e: e.tensor_tensor_scan(out=out, data0=d0, data1=d1, initial=init,
                                                          op0=ALU.mult, op1=ALU.add), R, W)

    def vcopy(out, in_, R, W):
        return S.op("dve", lambda e: e.tensor_copy(out=out, in_=in_), R, W)

    def memset(ap, val, W):
        return S.op("dve", lambda e: e.memset(ap, val), (), W)

    def mm(out, lhsT, rhs, start, stop, R, W):
        return S.op("pe", lambda e: e.matmul(out, lhsT=lhsT, rhs=rhs, start=start, stop=stop), R, W)

    def tr(out, in_, ident, R, W):
        return S.op("pe", lambda e: e.transpose(out, in_, ident), R, W)

    def dma_sp(key, out, in_, R, W):
        return S.dma("sp", key, lambda e: e.dma_start(out=out, in_=in_), R, W)

    def dma_pool(key, out, in_, R, W):
        return S.dma("pool", key, lambda e: e.dma_start(out=out, in_=in_), R, W)

    import os
    recording = useq_in is None
    useq = [] if recording else useq_in
    ring_state = {"issued": 0, "next": 0}

    def issue_unit(i):
        kind, a, b = useq[i]
        s = i % NSLOT
        slot, sbuf = RING[s], RINGb[s]
        key = f"ring{s}"
        xdep = []
        if kind == "gu":
            c0 = 256 * b
            for m, nm in enumerate(("g", "u")):
                src = wts[(nm, a)].rearrange("(kc p) n -> p kc n", p=128)[:, :, c0:c0 + 256]
                dst = slot[:, m * 2048:(m + 1) * 2048].rearrange("p (kc n) -> p kc n", n=256)
                dma_pool(key + "ab"[m], dst, src, xdep, [sbuf[m]])
        elif kind == "d":
            src = wts[("d", a)].rearrange("(f p) n -> p f n", p=128)[:, :, 128 * b:128 * b + 128]
            dst = slot[:, 0:NF * 128].rearrange("p (f n) -> p f n", n=128)
            dma_pool(key + "a", dst, src, (), sbuf)
        elif kind == "win":
            src = win_d.rearrange("(kc p) n -> p kc n", p=128)[:, :, a:a + b]
            dst = slot[:, 0:KC * b].rearrange("p (kc n) -> p kc n", n=b)
            dma_pool(key + "a", dst, src, (), sbuf)
        elif kind == "wo":
            src = wo_d.rearrange("(kc p) n -> p kc n", p=128)[:, :, 128 * a:128 * a + 128]
            dst = slot[:, 0:KC * 128].rearrange("p (kc n) -> p kc n", n=128)
            dma_pool(key + "a", dst, src, (), sbuf)

    def ring_next(spec):
        i = ring_state["next"]
        if recording:
            useq.append(spec)
        assert useq[i] == spec, (useq[i], spec)
        while ring_state["issued"] < min(len(useq), i + NSLOT - 1):
            issue_unit(ring_state["issued"])
            ring_state["issued"] += 1
        ring_state["next"] += 1
        s = i % NSLOT
        return RING[s], RINGb[s]

    if not recording:
        while ring_state["issued"] < min(len(useq), NSLOT - 1):
            issue_unit(ring_state["issued"])
            ring_state["issued"] += 1
    dma_sp("con", CON[:], consts_d, (), [CONb])
    dma_sp("vec", VEC[:], vecs_d, (), [VECb])
    dma_sp("h0", H0[:], slru_d, (), [H0b])
    dma_sp("scv", SCV[:], sconv_d, (), [SCVb])

    memset(BD[:], 0.0, [BDb])
    for a_i, src in enumerate((lwa_d, lwx_d)):
        sv = src.rearrange("(j two) i o -> two i j o", two=2)
        dma_pool("bd", BD[0:64, a_i, :, 0:64], sv[0], (), [BDb])
        dma_pool("bd", BD[64:128, a_i, :, 64:128], sv[1], (), [BDb])

    vcopy(IDB[:], CON[:, ID0:ID0 + 128], [CONb], [IDBb])
    memset(ONESB[:], 1.0, [ONESb])
    memset(DV[:, 36:37], EPS, [DVCb])
    memset(DV[:, 37:38], 1.0, [DVCb])
    memset(DV[:, 38:39], 0.0, [DVCb])
    def derive_consts():
        tt(DV[:, 0:4], VEC[:, 32:36], VEC[:, 36:40], ALU.subtract, [VECb], [DVb])
        act(DV[:, 0:4], DV[:, 0:4], AF.Tanh, [DVb], [DVb], scale=0.5)
        ts(DV[:, 4:8], DV[:, 0:4], 0.25, 0.75, ALU.mult, ALU.add, [DVb], [DVb])
        ts(DV[:, 8:12], DV[:, 0:4], -0.25, 0.25, ALU.mult, ALU.add, [DVb], [DVb])
        ts(DV[:, 12:16], DV[:, 0:4], 0.25, -0.25, ALU.mult, ALU.add, [DVb], [DVb])
        ts(DV[:, 16:20], VEC[:, 40:44], 0.5, None, ALU.mult, None, [VECb], [DVb])
        ts(DV[:, 20:24], VEC[:, 64:68], 0.5, None, ALU.mult, None, [VECb], [DVb])
        ts(DV[:, 24:28], VEC[:, 68:72], 0.5, None, ALU.mult, None, [VECb], [DVb])
        act(DV[:, 28:32], VEC[:, 72:76], AF.Exp, [VECb], [DVb], scale=-1.0)
        act(DV[:, 28:32], DV[:, 28:32], AF.Ln, [DVb, DVCb], [DVb], scale=1.0, bias=DV[:, 37:38])
        ts(DV[:, 32:36], DV[:, 28:32], -4.0, None, ALU.mult, None, [DVb], [DVb])
        ts(DV[:, 28:32], DV[:, 28:32], -8.0, None, ALU.mult, None, [DVb], [DVb])

    C0 = lambda h: DV[:, 4 + h:5 + h]
    C1 = lambda h: DV[:, 8 + h:9 + h]
    NC1 = lambda h: DV[:, 12 + h:13 + h]
    HG = lambda h: DV[:, 16 + h:17 + h]
    HBA = lambda j: DV[:, 20 + j:21 + j]
    HBX = lambda j: DV[:, 24 + j:25 + j]
    CC = lambda j: DV[:, 28 + j:29 + j]
    HCC = lambda j: DV[:, 32 + j:33 + j]
    EPSC = DV[:, 36:37]
    ONEC = DV[:, 37:38]

    MASKT = CON[:, MK0:MK0 + 64]
    M2 = CON[0:32, M20:M20 + 32]
    RESET = CON[:, RS0:RS0 + NT]
    SMASK = CON[:, SM0:SM0 + 256].rearrange("p (b t) -> p b t", t=32)
    IDF = CON[:, ID0:ID0 + 128]

    def norm_sq_one(kc, src, srcb, sq, sqb, pset):
        act(sq, src, AF.Square, [srcb], [sqb])
        for t in range(3):
            mm(PS[:, 3 * pset + t, 0:TL], ONESB[:], sq[:, t * TL:(t + 1) * TL], kc == 0, kc == KC - 1,
               [ONESb, sqb], [SETB[pset][t]])

    def norm_rstd(pset, rt, rtb):
        act(v3(rt), FFNV[pset], AF.Ln, SETB[pset] + [DVCb], [rtb], scale=1.0 / D, bias=EPSC)
        act(rt, rt, AF.Exp, [rtb], [rtb], scale=-0.5)

    def norm_apply_one(gcol, kc, src, srcb, rt, rtb, out, outbufs):
        stt(out, src, VEC[:, gcol + kc:gcol + kc + 1], rt, ALU.mult, ALU.mult, [srcb, VECb, rtb], outbufs)

    def square_ahead(kc):
        act(XN[:, kc, :], X[:, kc, :], AF.Square, [Xb[kc]], [XNb[kc]])

    def rmsnorm(gcol, out_fn, out_bufs_fn, pset, presq=False):
        for kc in range(KC):
            if presq:
                for t in range(3):
                    mm(PS[:, 3 * pset + t, 0:TL], ONESB[:], XN[:, kc, t * TL:(t + 1) * TL], kc == 0, kc == KC - 1,
                       [ONESb, XNb[kc]], [SETB[pset][t]])
                continue
            sq, sqb = (H[:, 11, :], Hb[11]) if kc % 2 == 0 else (H[:, 12, :], Hb[12])
            norm_sq_one(kc, X[:, kc, :], Xb[kc], sq, sqb, pset)
        norm_rstd(pset, T[3][:], Tb[3])
        for kc in range(KC):
            norm_apply_one(gcol, kc, X[:, kc, :], Xb[kc], T[3][:], Tb[3], out_fn(kc), out_bufs_fn(kc))

    def load_x(g, dst_fn, dst_bufs_fn, keypfx):
        for kc in range(KC):
            dma_sp(f"{keypfx}{kc}", dst_fn(kc), xT[g, :, kc, :], (), dst_bufs_fn(kc))

    def store_y(g):
        for kc in range(KC):
            out_events.append(dma_sp(f"y{kc}", yT[g, :, kc, :], X[:, kc, :], [Xb[kc]], ()))

    def ffn(which, gcol, pre_normed=False, hook1=None, hook2=None, presq=False, sq_ahead=False, late=False):
        if late:
            def xg(kc):
                if kc % 2 == 0:
                    act(XN[:, kc, :], X[:, kc, :], AF.Identity, [Xb[kc], VECb], [XNb[kc]],
                        scale=VEC[:, gcol + kc:gcol + kc + 1])
                else:
                    ts(XN[:, kc, :], X[:, kc, :], VEC[:, gcol + kc:gcol + kc + 1], None, ALU.mult, None,
                       [Xb[kc], VECb], [XNb[kc]])

            for kc in range(KC):
                if presq:
                    for t in range(3):
                        mm(PS[:, 3 + t, 0:TL], ONESB[:], XN[:, kc, t * TL:(t + 1) * TL], kc == 0, kc == KC - 1,
                           [ONESb, XNb[kc]], [SETB[1][t]])
                else:
                    sq, sqb = (H[:, 11, :], Hb[11]) if kc % 2 == 0 else (H[:, 12, :], Hb[12])
                    norm_sq_one(kc, X[:, kc, :], Xb[kc], sq, sqb, 1)
                    xg(kc)
            if presq:
                for kc in range(KC):
                    xg(kc)
            norm_rstd(1, T[3][:], Tb[3])
        elif not pre_normed:
            rmsnorm(gcol, lambda kc: XN[:, kc, :], lambda kc: [XNb[kc]], 0, presq=presq)
        for u in range(11):
            if hook1 is not None:
                hook1(u)
            slot, sbufs = ring_next(("gu", which, u))
            for fi in range(2):
                f = 2 * u + fi
                for m in range(2):
                    wv = slot[:, m * 2048:(m + 1) * 2048].rearrange("p (kc n) -> p kc n", n=256)
                    for kc in range(KC):
                        for t in range(3):
                            mm(PS[:, 3 * m + t, 0:TL], wv[:, kc, fi * 128:(fi + 1) * 128],
                               XN[:, kc, t * TL:(t + 1) * TL], kc == 0, kc == KC - 1,
                               [sbufs[m], XNb[kc]], [SETB[m][t]])
                tb = f % 2
                if late:
                    tt(v3(T[tb][:]), FFNV[0], v3(T[3][:]), ALU.mult, SETB[0] + [Tb[3]], [Tb[tb]])
                    act(T[tb][:], T[tb][:], AF.Silu, [Tb[tb]], [Tb[tb]])
                    tt(v3(T[tb][:]), v3(T[tb][:]), FFNV[1], ALU.mult, [Tb[tb]] + SETB[1], [Tb[tb]])
                    tt(H[:, f, :], T[tb][:], T[3][:], ALU.mult, [Tb[tb], Tb[3]], [Hb[f]])
                else:
                    act(v3(T[tb][:]), FFNV[0], AF.Silu, SETB[0], [Tb[tb]])
                    tt(v3(H[:, f, :]), v3(T[tb][:]), FFNV[1], ALU.mult, [Tb[tb]] + SETB[1], [Hb[f]])
        for o in range(8):
            if hook2 is not None:
                hook2(o)
            slot, sbufs = ring_next(("d", which, o))
            wv = slot[:, 0:NF * 128].rearrange("p (f n) -> p f n", n=128)
            ps = o % 2
            for f in range(NF):
                for t in range(3):
                    mm(PS[:, 3 * ps + t, 0:TL], wv[:, f, :], H[:, f, t * TL:(t + 1) * TL], f == 0, f == NF - 1,
                       sbufs + [Hb[f]], [SETB[ps][t]])
            stt(v3(X[:, o, :]), FFNV[ps], 0.5, v3(X[:, o, :]), ALU.mult, ALU.add, SETB[ps] + [Xb[o]], [Xb[o]])
            if sq_ahead:
                square_ahead(o)

    def prep_next_group(g_next):
        def hook(o):
            if o == 2:
                load_x(g_next, lambda kc: T[1 + kc][:], lambda kc: [Tb[1 + kc]], "xs")
            elif o == 3:
                for kc in range(KC):
                    act(XN[:, kc, :], T[1 + kc][:], AF.Square, [Tb[1 + kc]], [XNb[kc]])
            elif o == 4:
                pset = (o + 1) % 2
                for kc in range(KC):
                    for t in range(3):
                        mm(PS[:, 3 * pset + t, 0:TL], ONESB[:], XN[:, kc, t * TL:(t + 1) * TL], kc == 0, kc == KC - 1,
                           [ONESb, XNb[kc]], [SETB[pset][t]])
                norm_rstd(pset, T[0][:], Tb[0])
            elif o == 5:
                for kc in range(KC):
                    norm_apply_one(0, kc, T[1 + kc][:], Tb[1 + kc], T[0][:], Tb[0], XN[:, kc, :], [XNb[kc]])
        return hook

    def finish_prev_group(g_prev):
        def hook(u):
            if u == 1:
                for kc in range(KC):
                    sq, sqb = (H[:, 20, :], Hb[20]) if kc % 2 == 0 else (H[:, 21, :], Hb[21])
                    norm_sq_one(kc, X[:, kc, :], Xb[kc], sq, sqb, 1)
                norm_rstd(1, T[3][:], Tb[3])
            elif u == 2:
                for kc in range(KC):
                    norm_apply_one(24, kc, X[:, kc, :], Xb[kc], T[3][:], Tb[3], X[:, kc, :], [Xb[kc]])
                store_y(g_prev)
                load_x(g_prev + 1, lambda kc: X[:, kc, :], lambda kc: [Xb[kc]], "x")
        return hook

    TILES = ((0, 512), (512, 512), (1024, 32))

    def proj_flat(wv, sbuf, pset):
        for kc in range(KC):
            for t, (c0, n) in enumerate(TILES):
                mm(FLAT[pset][:, c0:c0 + n], wv[:, kc, :], XN[:, kc, c0:c0 + n], kc == 0, kc == KC - 1,
                   sbuf + [XNb[kc]], [SETB[pset][t]])

    def wslot(slot, n):
        return slot[:, 0:KC * n].rearrange("p (kc n) -> p kc n", n=n)

    class Stop(Exception):
        pass

    sub = int(os.environ.get("KSUB", "99"))

    def chk(n):
        if sub < n:
            raise Stop()

    fine = int(os.environ.get("KFINE", "99"))

    def chkf(n):
        if fine < n:
            raise Stop()

    def run_chains(gens, weights=None):
        active = list(gens)
        w = {id(gen): (weights[k] if weights else 1) for k, gen in enumerate(gens)}
        while active:
            for gen in list(active):
                for _ in range(w[id(gen)]):
                    try:
                        next(gen)
                    except StopIteration:
                        active.remove(gen)
                        break

    def proj_half(wv, sbuf, pset, tiles):
        for kc in range(KC):
            for (cc0, n, bk) in tiles:
                mm(FLAT[pset][:, cc0:cc0 + n], wv[:, kc, :], XN[:, kc, cc0:cc0 + n], kc == 0, kc == KC - 1,
                   sbuf + [XNb[kc]], [SETB[pset][bk]])

    def head_chain(g, h, hf, sh):
        c0, c1 = (0, 512) if hf == 0 else (512, NT)
        p1 = c0 + 512
        ck0 = 8 * hf
        samp = hf == 1
        tiles = [(0, 512, 0)] if hf == 0 else [(512, 512, 1), (1024, 32, 2)]
        A = FLAT[0]
        AB = [PSb[0]] if hf == 0 else [PSb[1], PSb[2]]
        B6 = [PSb[6]]
        ubanks = (0, 6) if hf == 0 else (1, 2)
        t0, t1, t2, t3 = (T[i][:, c0:c1] for i in range(4))
        b0, b1, b2, b3 = ([Tb[i][hf]] for i in range(4))
        qt, kt, kh, osq = QT[:, c0:c1], KT[:, c0:c1], KH[:, c0:c1], OSQ[:, c0:c1]
        qtb, ktb, khb, osqb = [QTb[hf]], [KTb[hf]], [KHb[hf]], [OSQb[hf]]
        Ah = A[:, c0:c1]
        if samp:
            dma_sp("s0", S0[:], shg_d[8 * g:8 * g + 8, h].rearrange("b k v -> k b v"), (), [S0b])
        if hf == 0:
            sh["fz"] = ring_next(("win", 512 + 128 * h, 128))
        slot, sbuf = sh["fz"]
        proj_half(wslot(slot, 128), sbuf, 0, tiles)
        yield
        act(t0, Ah, AF.Tanh, AB, b0, scale=0.5)
        yield
        ts(t1, t0, C1(h), C0(h), ALU.mult, ALU.add, b0 + [DVb], b1)
        ts(t2, t0, NC1(h), C1(h), ALU.mult, ALU.add, b0 + [DVb], b2)
        yield
        act(t0, t1, AF.Ln, b1, b0)
        if hf == 0:
            sh["q"] = ring_next(("win", 128 * h, 128))
        slot, sbuf = sh["q"]
        proj_half(wslot(slot, 128), sbuf, 0, tiles)
        yield
        scan(t1, RESET[:, c0:c1], t0, 0.0, [CONb] + b0, b1)
        yield
        act(t0, t1, AF.Exp, b1, b0)
        act(t3, t1, AF.Exp, b1, b3, scale=-1.0)
        yield
        tt(t3, t2, t3, ALU.mult, b2 + b3, b3)
        yield
        e_p = T[0][:, c0:p1].rearrange("p (c t) -> p c t", t=64)
        dP = e_p[:, :, 63:64].to_broadcast([128, 8, 64])
        tt(KH[:, c0:p1].rearrange("p (c t) -> p c t", t=64), T[3][:, c0:p1].rearrange("p (c t) -> p c t", t=64),
           dP, ALU.mult, b3 + b0, khb)
        if samp:
            dS = T[0][:, NP:NT].rearrange("p (c t) -> p c t", t=4)[:, :, 3:4].to_broadcast([128, 8, 4])
            tt(KH[:, NP:NT].rearrange("p (c t) -> p c t", t=4), T[3][:, NP:NT].rearrange("p (c t) -> p c t", t=4),
               dS, ALU.mult, b3 + b0, khb)
        yield
        act(t2, Ah, AF.Tanh, AB, b2, scale=0.5)
        yield
        stt(t2, t2, 1.0, Ah, ALU.add, ALU.mult, b2 + AB, b2)
        for bl in range(4):
            blk = 4 * hf + bl
            tr(MISCBF[0][:, blk * 128:(blk + 1) * 128], KH[:, blk * 128:(blk + 1) * 128], IDB[:],
               khb + [IDBb], B6)
        yield
        stt(qt, t2, 0.5, t0, ALU.mult, ALU.mult, b2 + b0, qtb)
        if hf == 0:
            sh["g"] = ring_next(("win", 1536 + 128 * h, 128))
        slot, sbuf = sh["g"]
        proj_half(wslot(slot, 128), sbuf, 0, tiles)
        yield
        act(t1, Ah, AF.Tanh, AB, b1, scale=0.5)
        acopy(KTOK[:, 4 * hf:4 * hf + 4, :].rearrange("p a b -> p (a b)"),
              MISCBF[0][:, 512 * hf:512 * hf + 512], B6, [KTOKb[hf]])
        acopy(kt, t3, b3, ktb)
        if samp:
            tt(KHM[:], KH[:, NP:NT].unsqueeze(1).to_broadcast([128, 8, 32]), SMASK, ALU.mult, khb + [CONb], [KHMb])
        yield
        if samp:
            for b in range(8):
                tr(MISCBF[0][0:32, b * 128:(b + 1) * 128], KHM[:, b, :], IDB[:], [KHMb, IDBb], B6)
        yield
        if samp:
            acopy(KTOKS[:].rearrange("p a b -> p (a b)"), MISCBF[0][0:32, 0:1024], B6, [KTOKSb])
            acopy(S0BF[:].rearrange("p a b -> p (a b)"), S0[:].rearrange("p a b -> p (a b)"), [S0b], [S0BFb])
        if hf == 0:
            if g == 0:
                memset(SALL[:, 0, :], 0.0, [SALLb[0]])
            else:
                vcopy(SALL[:, 0, :], SCAR[:, h, :], [SCARb], [SALLb[0]])
        yield
        for cl in range(8 if "chain" not in CFG["skip"] else 0):
            c = ck0 + cl
            r0 = 64 * (c % 2)
            sl = ubanks[c % 2]
            pu = PS[:, sl, 0:128]
            mm(pu, KTOK[r0:r0 + 64, c // 2, :], VTOK[r0:r0 + 64, c // 2, 128 * h:128 * h + 128], True, True,
               [KTOKb[hf]] + VTOKb, [PSb[sl]])
            stt(SALL[:, c + 1, :], SALL[:, c, :], T[0][:, 64 * c + 63:64 * c + 64], pu, ALU.mult, ALU.add,
                SALLb + b0 + [PSb[sl]], [SALLb[hf]])
        yield
        acopy(SBF[:, ck0:ck0 + 8, :].rearrange("p a b -> p (a b)"),
              SALL[:, ck0:ck0 + 8, :].rearrange("p a b -> p (a b)"), SALLb, [SBFb[hf]])
        if samp:
            vcopy(SCAR[:, h, :], SALL[:, 16, :], [SALLb[1]], [SCARb])
            if g == G - 1:
                out_events.append(dma_sp("hgp", hgp_d[h], SCAR[:, h, :], [SCARb], ()))
        for cl in range(8):
            c = ck0 + cl
            r0 = 64 * (c % 2)
            mm(PS[r0:r0 + 64, 6, (c // 2) * 64:(c // 2) * 64 + 64], kt[:, 64 * cl:64 * cl + 64],
               qt[:, 64 * cl:64 * cl + 64], True, True, ktb + qtb, B6)
        yield
        tt(AT[:, 4 * hf:4 * hf + 4, :], PS[:, 6, 256 * hf:256 * hf + 256].rearrange("p (a b) -> p a b", b=64),
           MASKT.unsqueeze(1).to_broadcast([128, 4, 64]), ALU.mult, B6 + [CONb], [ATb[hf]])
        yield
        if samp:
            mm(PS[0:32, 6, 0:32], KT[:, NP:NT], QT[:, NP:NT], True, True, ktb + qtb, B6)
        yield
        if samp:
            tt(ATS[:], PS[0:32, 6, 0:32], M2, ALU.mult, B6 + [CONb], [ATSb])
        for cl in range(8):
            c = ck0 + cl
            r0 = 64 * (c % 2)
            bk = [PSb[0]] if hf == 0 else [PSb[1]]
            mm(A[:, 64 * c:64 * c + 64], VTOK[r0:r0 + 64, c // 2, 128 * h:128 * h + 128], AT[r0:r0 + 64, c // 2, :],
               True, False, VTOKb + [ATb[hf]], bk)
            mm(A[:, 64 * c:64 * c + 64], SBF[:, c, :], QT[:, 64 * c:64 * c + 64], False, True, [SBFb[hf]] + qtb, bk)
        yield
        if samp:
            mm(A[:, NP:NT], VTOK[0:32, 8, 128 * h:128 * h + 128], ATS[:], True, False, VTOKb + [ATSb], [PSb[2]])
            for b in range(8):
                mm(A[:, NP + 4 * b:NP + 4 * b + 4], S0BF[:, b, :], QT[:, NP + 4 * b:NP + 4 * b + 4], False, b == 7,
                   [S0BFb] + qtb, [PSb[2]])
        yield
        act(osq, Ah, AF.Square, AB, osqb)
        act(t3, Ah, AF.Identity, AB + [DVb], b3, scale=HG(h))
        yield
        for (cc0, n, bk) in tiles:
            mm(A[:, cc0:cc0 + n], ONESB[:], OSQ[:, cc0:cc0 + n], True, True, [ONESb] + osqb, [SETB[0][bk]])
        stt(t3, t1, 1.0, t3, ALU.add, ALU.mult, b1 + b3, b3)
        yield
        act(t2, Ah, AF.Ln, AB + [DVCb], b2, scale=1.0 / 128, bias=EPSC)
        yield
        act(t2, t2, AF.Exp, b2, b2, scale=-0.5)
        yield
        yield
        tt(OA[:, h, c0:c1], t3, t2, ALU.mult, b3 + b2, [Hb[h]])
        if samp:
            for b in range(8):
                mm(PS[:, 1 + b // 4, (b % 4) * 128:(b % 4) * 128 + 128], KTOKS[0:32, b, :],
                   VTOK[0:32, 8, 128 * h:128 * h + 128], True, True, [KTOKSb] + VTOKb, [PSb[1 + b // 4]])
            dB = T[0][:, NP:NT].rearrange("p (b t) -> p b t", t=4)[:, :, 3:4].to_broadcast([128, 8, 128])
            tt(SOUT[:], S0[:], dB, ALU.mult, [S0b] + b0, [SOUTb])
            tt(SOUT[:].rearrange("p b v -> p (b v)"), SOUT[:].rearrange("p b v -> p (b v)"),
               PS[:, 1:3, :].rearrange("p a b -> p (a b)"), ALU.add, [SOUTb, PSb[1], PSb[2]], [SOUTb])
            out_events.append(dma_sp("hgs", hgs_d[8 * g:8 * g + 8, h].rearrange("b k v -> k b v"), SOUT[:],
                                     [SOUTb], ()))

    def lru_chain(g, j, hf, sh):
        c0, c1 = (0, 512) if hf == 0 else (512, NT)
        p1 = c0 + 512
        samp = hf == 1
        tiles = [(0, 512, 0)] if hf == 0 else [(512, 512, 1), (1024, 32, 2)]
        Bf = FLAT[1]
        BB = [PSb[3]] if hf == 0 else [PSb[4], PSb[5]]
        B7 = [PSb[7]]
        l0, l1, l2, l3, l4 = (T[i][:, c0:c1] for i in range(4, 9))
        lb0, lb1, lb2, lb3, lb4 = ([Tb[i][hf]] for i in range(4, 9))
        L0, L1, L2, L4 = T[4], T[5], T[6], T[8]
        xcb, xcbb = XCB[:, c0:c1], [XCBb[hf]]
        Bh = Bf[:, c0:c1]
        xbw = [XBb[hf]]
        xbr = XBb
        if hf == 0:
            sh["xb"] = ring_next(("win", 2048 + 128 * j, 128))
        slot, sbuf = sh["xb"]
        proj_half(wslot(slot, 128), sbuf, 1, tiles)
        if hf == 0:
            if g == 0:
                memset(XB[:, 0:3], 0.0, [XBb[0]])
            else:
                vcopy(XB[:, 0:3], CONVC[:, j, :], [CONVCb], [XBb[0]])
        else:
            vcopy(XBS[:, :, 0:3], SCV[:, j, 8 * g:8 * g + 8, :], [SCVb], [XBSb])
        yield
        acopy(XB[:, 3 + c0:3 + p1], Bf[:, c0:p1], BB, xbw)
        if samp:
            acopy(XBS[:, :, 3:7], Bf[:, NP:NT].rearrange("p (b t) -> p b t", t=4), BB, [XBSb])
        yield
        if samp:
            vcopy(CONVC[:, j, :], XB[:, NP:NP + 3], xbr, [CONVCb])
        cw = lambda tap: VEC[:, 44 + 4 * j + tap:45 + 4 * j + tap]
        ts(L4[:, c0:p1], XB[:, 3 + c0:3 + p1], cw(3), VEC[:, 60 + j:61 + j], ALU.mult, ALU.add, xbr + [VECb], lb4)
        L4s = L4[:, NP:NT].rearrange("p (b t) -> p b t", t=4)
        if samp:
            ts(L4s, XBS[:, :, 3:7], cw(3), VEC[:, 60 + j:61 + j], ALU.mult, ALU.add, [XBSb, VECb], lb4)
        yield
        for tap in range(3):
            stt(L4[:, c0:p1], XB[:, tap + c0:tap + p1], cw(tap), L4[:, c0:p1], ALU.mult, ALU.add,
                xbr + [VECb] + lb4, lb4)
            if samp:
                stt(L4s, XBS[:, :, tap:tap + 4], cw(tap), L4s, ALU.mult, ALU.add, [XBSb, VECb] + lb4, lb4)
            yield
        acopy(xcb, l4, lb4, xcbb)
        yield
        for (cc0, n, bk) in tiles:
            mm(Bf[:, cc0:cc0 + n], BD[:, 0, j, :], XCB[:, cc0:cc0 + n], True, True, [BDb] + xcbb, [SETB[1][bk]])
        yield
        act(l0, Bh, AF.Tanh, BB + [DVb], lb0, scale=0.5, bias=HBA(j))
        yield
        for (cc0, n, bk) in tiles:
            mm(Bf[:, cc0:cc0 + n], BD[:, 1, j, :], XCB[:, cc0:cc0 + n], True, True, [BDb] + xcbb, [SETB[1][bk]])
        act(l2, l0, AF.Exp, lb0 + [DVb], lb2, scale=HCC(j), bias=HCC(j))
        act(l3, l0, AF.Exp, lb0 + [DVb], lb3, scale=CC(j), bias=CC(j))
        yield
        act(l1, Bh, AF.Tanh, BB + [DVb], lb1, scale=0.5, bias=HBX(j))
        if hf == 0:
            sh["yb"] = ring_next(("win", 2560 + 128 * j, 128))
        slot, sbuf = sh["yb"]
        proj_half(wslot(slot, 128), sbuf, 1, tiles)
        yield
        act(l3, l3, AF.Ln, lb3 + [DVb, DVCb], lb3, scale=-1.0, bias=ONEC)
        stt(l1, l1, 1.0, l4, ALU.add, ALU.mult, lb1 + lb4, lb1)
        yield
        act(l3, l3, AF.Exp, lb3, lb3, scale=0.5)
        if g == 0 and hf == 0:
            memset(T[7][:, 0:1], 1.0, lb3)
        yield
        stt(l1, l1, 0.5, l3, ALU.mult, ALU.mult, lb1 + lb3, lb1)
        a_s = L2[:, NP:NT].rearrange("p (b t) -> p b t", t=4)
        u_s = L1[:, NP:NT].rearrange("p (b t) -> p b t", t=4)
        if samp:
            tt(TMP8[:].unsqueeze(2), a_s[:, :, 0:1], H0[:, j, 8 * g:8 * g + 8].unsqueeze(2), ALU.mult,
               lb2 + [H0b], [TMP8b])
        yield
        if samp:
            tt(u_s[:, :, 0:1], u_s[:, :, 0:1], TMP8[:].unsqueeze(2), ALU.add, lb1 + [TMP8b], lb1)
            memset(a_s[:, :, 0:1], 0.0, lb2)
        yield
        if hf == 0:
            init, initb = (0.0, []) if g == 0 else (HC[:, j:j + 1], [HCb])
        else:
            init, initb = L0[:, 511:512], [Tb[4][0]]
        scan(L0[:, c0:p1], L2[:, c0:p1], L1[:, c0:p1], init, lb2 + lb1 + initb, lb0)
        if samp:
            scan(L0[:, NP:NT], L2[:, NP:NT], L1[:, NP:NT], 0.0, lb2 + lb1, lb0)
        yield
        if samp:
            vcopy(HC[:, j:j + 1], L0[:, NP - 1:NP], lb0, [HCb])
        PY = Bh
        act(l2, PY, AF.Square, BB, lb2)
        yield
        ts(l2, l2, 0.044715, 1.0, ALU.mult, ALU.add, lb2, lb2)
        yield
        tt(l2, l2, PY, ALU.mult, lb2 + BB, lb2)
        yield
        act(l2, l2, AF.Tanh, lb2, lb2, scale=0.7978845608028654)
        if samp:
            vcopy(OUTF[:, 0:24].rearrange("p (b t) -> p b t", t=3), XBS[:, :, 4:7], [XBSb], [OUTFb])
            vcopy(OUTF[:, 24:32].unsqueeze(2), L0[:, NP:NT].rearrange("p (b t) -> p b t", t=4)[:, :, 3:4],
                  lb0, [OUTFb])
            vcopy(OUTF[:, 32:35], XB[:, NP:NP + 3], xbr, [OUTFb])
            vcopy(OUTF[:, 35:36], L0[:, NP - 1:NP], lb0, [OUTFb])
        yield
        if samp:
            tr(PS[0:36, 7, 0:128], OUTF[:], IDF, [OUTFb, CONb], B7)
        stt(l2, l2, 1.0, PY, ALU.add, ALU.mult, lb2 + BB, lb2)
        yield
        if samp:
            acopy(OUTT[:], PS[0:36, 7, 0:128], B7, [OUTTb])
            out_events.append(dma_sp("small", small_d[g, j], OUTT[:], [OUTTb], ()))
        stt(OA[:, 4 + j, c0:c1], l2, 0.5, l0, ALU.mult, ALU.mult, lb2 + lb0, [Hb[4 + j]])

    def mixer(g):
        rmsnorm(8, lambda kc: XN[:, kc, :], lambda kc: [XNb[kc]], 0, presq=True)
        Sched.fence(MIXTMP, Hb[8:22])
        slot, sbuf = ring_next(("win", 1024, 512))
        wv = wslot(slot, 512)
        for kc in range(KC):
            for blk in range(8):
                mm(PS[:, blk, :], XN[:, kc, blk * 128:blk * 128 + 128], wv[:, kc, :], kc == 0, kc == KC - 1,
                   sbuf + [XNb[kc]], [PSb[blk]])
        for blk in range(8):
            acopy(VTOK[:, blk, :], PS[:, blk, :], [PSb[blk]], VTOKb)
        for kc in range(KC):
            mm(PS[0:32, 0, :], XN[:, kc, NP:NT], wv[:, kc, :], kc == 0, kc == KC - 1, sbuf + [XNb[kc]], [PSb[0]])
        acopy(VTOK[0:32, 8, :], PS[0:32, 0, :], [PSb[0]], VTOKb)
        def build(hgens, lgens):
            wh, wl = CFG["wh"], CFG["wl"]
            if CFG["mode"] == "head":
                return hgens, [wh, wh]
            if CFG["mode"] == "lru":
                return lgens, [wl, wl]
            if CFG["mode"] == "none":
                return [], []
            o = CFG["order"]
            if o == "hhll":
                return hgens + lgens, [wh, wh, wl, wl]
            if o == "llhh":
                return lgens + hgens, [wl, wl, wh, wh]
            if o == "hlhl":
                return [hgens[0], lgens[0], hgens[1], lgens[1]], [wh, wl, wh, wl]
            raise ValueError(o)

        if CFG["streams"]:
            shh = [{} for _ in range(4)]
            shl = [{} for _ in range(4)]

            def seq(fn, hf, shs):
                for i in range(4):
                    yield from fn(g, i, hf, shs[i])

            gens, ws = build([seq(head_chain, 0, shh), seq(head_chain, 1, shh)],
                             [seq(lru_chain, 0, shl), seq(lru_chain, 1, shl)])
            run_chains(gens, ws)
        else:
            for i in range(4):
                shh, shl = {}, {}
                gens, ws = build([head_chain(g, i, 0, shh), head_chain(g, i, 1, shh)],
                                 [lru_chain(g, i, 0, shl), lru_chain(g, i, 1, shl)])
                run_chains(gens, ws)
        Sched.fence(Hb[8:22], MIXTMP)
        chk(15)
        for o in range(8):
            slot, sbuf = ring_next(("wo", o, 128))
            wv = wslot(slot, 128)
            ps = o % 2
            for kc in range(KC):
                for t in range(3):
                    mm(PS[:, 3 * ps + t, 0:TL], wv[:, kc, :], OA[:, kc, t * TL:(t + 1) * TL], kc == 0, kc == KC - 1,
                       sbuf + [Hb[kc]], [SETB[ps][t]])
            tt(v3(X[:, o, :]), FFNV[ps], v3(X[:, o, :]), ALU.add, SETB[ps] + [Xb[o]], [Xb[o]])
            square_ahead(o)

    load_x(0, lambda kc: X[:, kc, :], lambda kc: [Xb[kc]], "x")
    for g in range(G):
        last = (g == G - 1)
        if g == 0:
            ffn(1, 0, sq_ahead=True, late=True)
            derive_consts()
        else:
            ffn(1, 0, pre_normed=True, hook1=finish_prev_group(g - 1), sq_ahead=True)
        mixer(g)
        ffn(2, 16, hook2=None if last else prep_next_group(g + 1), presq=True, sq_ahead=last, late=True)
    rmsnorm(24, lambda kc: X[:, kc, :], lambda kc: [Xb[kc]], 0, presq=True)
    store_y(G - 1)
    S.wait_all("sp", out_events)
    if recording:
        return useq
    S.emit()
    return nc


def _consts():
    c = np.zeros((128, NCONST), np.float32)
    c[:, ID0:ID0 + 128] = np.eye(128, dtype=np.float32)
    p = np.arange(128)[:, None] % 64
    t = np.arange(64)[None, :]
    c[:, MK0:MK0 + 64] = (p <= t).astype(np.float32)
    s = np.arange(32)[:, None]
    t = np.arange(32)[None, :]
    c[0:32, M20:M20 + 32] = ((s // 4 == t // 4) & (s <= t)).astype(np.float32)
    r = np.ones(NT, np.float32)
    r[0:NP:64] = 0.0
    r[NP:NT:4] = 0.0
    c[:, RS0:RS0 + NT] = r[None, :]
    b = np.arange(8)[:, None]
    tok = np.arange(32)[None, :]
    c[:, SM0:SM0 + 256] = (tok // 4 == b).astype(np.float32).reshape(1, 256)
    return c


def _fm(v, n):
    return np.ascontiguousarray(np.asarray(v, np.float32).reshape(n, 128).T)


_NC_CACHE = {}


def prepare(x_prompt, x_sample, state_hgrn, state_lru, state_conv, ffn1_norm, ffn1_wg, ffn1_wu, ffn1_wd,
            mix_norm, w_in, hgrn_lb, hgrn_norm, conv_w, conv_b, lru_wa, lru_ba, lru_wx, lru_bx, lru_lambda,
            w_o, ffn2_norm, ffn2_wg, ffn2_wu, ffn2_wd, final_norm):
    f32 = np.float32
    x_prompt = np.asarray(x_prompt, f32)
    x_sample = np.asarray(x_sample, f32)
    state_hgrn = np.asarray(state_hgrn, f32)
    state_lru = np.asarray(state_lru, f32)
    state_conv = np.asarray(state_conv, f32)

    vecs = np.zeros((128, NV), f32)
    vecs[:, 0:8] = _fm(ffn1_norm[0], 8)
    vecs[:, 8:16] = _fm(mix_norm[0], 8)
    vecs[:, 16:24] = _fm(ffn2_norm[0], 8)
    vecs[:, 24:32] = _fm(final_norm, 8)
    vecs[:, 32:36] = _fm(hgrn_lb[0], 4)
    vecs[:, 36:40] = _fm(hgrn_lb[1], 4)
    vecs[:, 40:44] = _fm(hgrn_norm[0], 4)
    cw = np.asarray(conv_w, f32)[0]
    for j in range(4):
        for tap in range(4):
            vecs[:, 44 + 4 * j + tap] = cw[tap, 128 * j:128 * j + 128]
    vecs[:, 60:64] = _fm(conv_b[0], 4)
    vecs[:, 64:68] = _fm(np.asarray(lru_ba, f32)[0].reshape(512), 4)
    vecs[:, 68:72] = _fm(np.asarray(lru_bx, f32)[0].reshape(512), 4)
    vecs[:, 72:76] = _fm(lru_lambda[0], 4)
    consts = _consts()

    shared = {
        "consts": consts, "vecs": vecs,
        "wg1": np.ascontiguousarray(ffn1_wg[0], f32), "wu1": np.ascontiguousarray(ffn1_wu[0], f32),
        "wd1": np.ascontiguousarray(ffn1_wd[0], f32),
        "wg2": np.ascontiguousarray(ffn2_wg[0], f32), "wu2": np.ascontiguousarray(ffn2_wu[0], f32),
        "wd2": np.ascontiguousarray(ffn2_wd[0], f32),
        "win": np.ascontiguousarray(w_in[0], f32), "wo": np.ascontiguousarray(w_o[0], f32),
        "lwa": np.ascontiguousarray(lru_wa[0], f32), "lwx": np.ascontiguousarray(lru_wx[0], f32),
    }
    in_maps = []
    for c in range(NCORES):
        xs = x_sample[16 * c:16 * c + 16]
        xt = np.empty((G, 128, KC, NT), f32)
        for g in range(G):
            tok = np.concatenate([x_prompt[c, g * NP:(g + 1) * NP], xs[8 * g:8 * g + 8].reshape(NS, D)], axis=0)
            xt[g] = tok.T.reshape(KC, 128, NT).transpose(1, 0, 2)
        m = dict(shared)
        m["xT"] = xt
        m["s_hgrn"] = np.ascontiguousarray(state_hgrn[0, 16 * c:16 * c + 16])
        m["s_lruT"] = np.ascontiguousarray(state_lru[0, 16 * c:16 * c + 16].T.reshape(4, 128, 16).transpose(1, 0, 2))
        sc = state_conv[0, 16 * c:16 * c + 16]
        m["s_convT"] = np.ascontiguousarray(sc.transpose(2, 0, 1).reshape(4, 128, 16, 3).transpose(1, 0, 2, 3))
        in_maps.append(m)
    return in_maps


def assemble(results):
    f32 = np.float32
    y_prompt = np.empty((8, 2048, D), f32)
    y_sample = np.empty((128, 4, D), f32)
    hgrn_p = np.empty((1, 8, 4, 128, 128), f32)
    lru_p = np.empty((1, 8, 512), f32)
    conv_p = np.empty((1, 8, 3, 512), f32)
    hgrn_s = np.empty((1, 128, 4, 128, 128), f32)
    lru_s = np.empty((1, 128, 512), f32)
    conv_s = np.empty((1, 128, 3, 512), f32)
    for c in range(NCORES):
        r = results[c]
        yt = np.asarray(r["yT"], f32)
        for g in range(G):
            tok = yt[g].transpose(1, 0, 2).reshape(D, NT).T
            y_prompt[c, g * NP:(g + 1) * NP] = tok[0:NP]
            y_sample[16 * c + 8 * g:16 * c + 8 * g + 8] = tok[NP:NT].reshape(8, 4, D)
        hgrn_p[0, c] = np.asarray(r["hgrn_p"], f32)
        hgrn_s[0, 16 * c:16 * c + 16] = np.asarray(r["hgrn_s"], f32)
        sm = np.asarray(r["small"], f32)
        for g in range(G):
            for j in range(4):
                blk = sm[g, j]
                fs = slice(128 * j, 128 * j + 128)
                conv_s[0, 16 * c + 8 * g:16 * c + 8 * g + 8, :, fs] = blk[0:24].reshape(8, 3, 128)
                lru_s[0, 16 * c + 8 * g:16 * c + 8 * g + 8, fs] = blk[24:32]
                if g == G - 1:
                    conv_p[0, c, :, fs] = blk[32:35]
                    lru_p[0, c, fs] = blk[35]
    return (y_prompt, y_sample, hgrn_p, lru_p, conv_p, hgrn_s, lru_s, conv_s)


def kernel(**inputs):
    in_maps = prepare(**inputs)
    if "nc" not in _NC_CACHE:
        _NC_CACHE["nc"] = build_program(build_program())
    nc = _NC_CACHE["nc"]
    res = run_bass_kernel_spmd(nc, in_maps, core_ids=list(range(NCORES)))
    return assemble(res.results)
```
